# Optimizing a Trainium2 kernel written in Bass

```python
import functools
import jax, jax.numpy as jnp
from jax import lax
import numpy as np

D_MODEL = 1024
BATCH = 32
SEQ = 256
DEPTH = 4
DEC_BATCH = 2
DEC_SEQ = 2048
PAST_LEN = 512

GRID_W = 64
N_MIXERS = 2
N_RG_LAYERS = (DEPTH + 1) // 2
N_POOL_LAYERS = DEPTH // 2
D_RNN = D_MODEL
RNN_HEADS = 8
RNN_HEAD_DIM = D_RNN // RNN_HEADS
CONV_W = 4
RG_C = 8.0
POOL_WINDOWS = (2, 4, 8, 16)
POOL_GROUPS = 4
POOL_GROUP_DIM = D_MODEL // POOL_GROUPS
D_FF = 2816
N_MOD = 9
EPS = 1e-6

kernel_name = 'hybrid_rglru_pool_diffusion_step'


def _rmsnorm(x, g):
    xf = x.astype(jnp.float32)
    y = xf * lax.rsqrt(jnp.mean(xf * xf, axis=-1, keepdims=True) + EPS)
    return (y * g.astype(jnp.float32)).astype(x.dtype)


def _modulate(x, shift, scale):
    return x * (1 + scale) + shift


def _swiglu(x, w1, w3, w2):
    return (jax.nn.silu(x @ w1) * (x @ w3)) @ w2


def _dwconv_centred(x, w, b):
    n = x.shape[1]
    left = (CONV_W - 1) // 2
    right = CONV_W - 1 - left
    xp = jnp.pad(x, ((0, 0), (left, right), (0, 0)))
    out = b
    for k in range(CONV_W):
        out = out + xp[:, k:k + n] * w[k]
    return out


def _lru_combine(earlier, later):
    a1, b1 = earlier
    a2, b2 = later
    return a1 * a2, a2 * b1 + b2


def _rglru_scan(x, w_a, b_a, w_x, b_x, lam, h0):
    bsz, n, _ = x.shape
    xh = x.reshape(bsz, n, RNN_HEADS, RNN_HEAD_DIM)
    r = jax.nn.sigmoid(jnp.einsum('bnhi,hij->bnhj', xh, w_a).reshape(bsz, n, D_RNN) + b_a)
    i = jax.nn.sigmoid(jnp.einsum('bnhi,hij->bnhj', xh, w_x).reshape(bsz, n, D_RNN) + b_x)
    log_a = -RG_C * r.astype(jnp.float32) * jax.nn.softplus(-lam.astype(jnp.float32))
    a = jnp.exp(log_a)
    beta = jnp.sqrt(-jnp.expm1(2.0 * log_a))
    u = beta * (i * x).astype(jnp.float32)
    u = u.at[:, 0].add(a[:, 0] * h0.astype(jnp.float32))
    _, h = lax.associative_scan(_lru_combine, (a, u), axis=1)
    return h.astype(x.dtype), h[:, -1].astype(x.dtype)


def _rg_mixer(h, p, j, h0_fwd, h0_bwd):
    w_in, conv_w, conv_b, w_a, b_a, w_x, b_x, lam, w_out = p
    gate, xr = jnp.split(h @ w_in[j], 2, axis=-1)
    xr = _dwconv_centred(xr, conv_w[j], conv_b[j])
    y_f, s_f = _rglru_scan(xr, w_a[j, 0], b_a[j, 0], w_x[j, 0], b_x[j, 0], lam[j, 0], h0_fwd)
    y_b, s_b = _rglru_scan(xr[:, ::-1], w_a[j, 1], b_a[j, 1], w_x[j, 1], b_x[j, 1], lam[j, 1], h0_bwd)
    y = (y_f + y_b[:, ::-1]) * jax.nn.gelu(gate)
    return y @ w_out[j], jnp.stack([s_f, s_b], axis=1)


def _window_bounds(n, w):
    t = jnp.arange(n)
    lo = jnp.maximum(t - w // 2, 0)
    hi = jnp.minimum(t + (w - w // 2), n)
    return lo, hi


def _pool1d_mean(x, w):
    n = x.shape[1]
    p = jnp.pad(jnp.cumsum(x.astype(jnp.float32), axis=1), ((0, 0), (1, 0), (0, 0)))
    lo, hi = _window_bounds(n, w)
    s = p[:, hi] - p[:, lo]
    return s / (hi - lo).astype(jnp.float32)[None, :, None]


def _pool2d_mean(x, w):
    bsz, n, cg = x.shape
    rows = n // GRID_W
    g = x.astype(jnp.float32).reshape(bsz, rows, GRID_W, cg)
    s = jnp.cumsum(jnp.cumsum(g, axis=1), axis=2)
    s = jnp.pad(s, ((0, 0), (1, 0), (1, 0), (0, 0)))
    rlo, rhi = _window_bounds(rows, w)
    clo, chi = _window_bounds(GRID_W, w)
    sr = s[:, rhi] - s[:, rlo]
    box = sr[:, :, chi] - sr[:, :, clo]
    cnt = ((rhi - rlo)[:, None] * (chi - clo)[None, :]).astype(jnp.float32)[None, :, :, None]
    return (box / cnt).reshape(bsz, n, cg)


def _pool_mixer(h, p, j, on_grid):
    pool_w, pool_scale = p
    bsz, n, _ = h.shape
    hg = h.reshape(bsz, n, POOL_GROUPS, POOL_GROUP_DIM)
    outs = []
    for g, w in enumerate(POOL_WINDOWS):
        xg = hg[:, :, g]
        m = _pool2d_mean(xg, w) if on_grid else _pool1d_mean(xg, w)
        outs.append((m.astype(h.dtype) - xg) @ pool_w[j, g])
    return jnp.concatenate(outs, axis=-1) * pool_scale[j], None


def _layer(x, mod, l, norm_g, ffn_w1, ffn_w3, ffn_w2, mixer):
    sh1, sc1, g1, sh2, sc2, g2, sh3, sc3, g3 = jnp.split(mod[:, None, :].astype(x.dtype), N_MOD, axis=-1)
    h = _modulate(_rmsnorm(x, norm_g[l, 0]), sh1, sc1)
    x = x + 0.5 * g1 * _rmsnorm(_swiglu(h, ffn_w1[l, 0], ffn_w3[l, 0], ffn_w2[l, 0]), norm_g[l, 1])
    h = _modulate(_rmsnorm(x, norm_g[l, 2]), sh2, sc2)
    y, st = mixer(h)
    x = x + g2 * _rmsnorm(y, norm_g[l, 3])
    h = _modulate(_rmsnorm(x, norm_g[l, 4]), sh3, sc3)
    x = x + 0.5 * g3 * _rmsnorm(_swiglu(h, ffn_w1[l, 1], ffn_w3[l, 1], ffn_w2[l, 1]), norm_g[l, 5])
    return x, st


def setup_inputs(seed: int = 0) -> dict:
    key = jax.random.key(seed)
    ks = jax.random.split(key, 24)
    f32 = jnp.float32
    nrm = lambda k, shape, s: jax.random.normal(k, shape, f32) * s
    u = jax.random.uniform(ks[20], (N_RG_LAYERS, 2, D_RNN), f32, minval=0.9, maxval=0.999)
    base = u ** (1.0 / RG_C)
    lam = jnp.log(base) - jnp.log1p(-base)
    return {
        'x_prompt': nrm(ks[0], (BATCH, SEQ, D_MODEL), 1.0),
        'x_sample': nrm(ks[1], (DEC_BATCH, DEC_SEQ, D_MODEL), 1.0),
        'state_rglru': nrm(ks[2], (DEC_BATCH, N_RG_LAYERS, 2, D_RNN), 0.5),
        'c': nrm(ks[3], (DEC_BATCH, D_MODEL), 1.0),
        'c_ctx': nrm(ks[4], (D_MODEL,), 1.0),
        'mod_w': nrm(ks[5], (DEPTH, D_MODEL, N_MOD * D_MODEL), 0.3 * D_MODEL ** -0.5),
        'mod_b': nrm(ks[6], (DEPTH, N_MOD * D_MODEL), 0.02),
        'norm_g': 1.0 + nrm(ks[7], (DEPTH, 6, D_MODEL), 0.05),
        'ffn_w1': nrm(ks[8], (DEPTH, 2, D_MODEL, D_FF), D_MODEL ** -0.5),
        'ffn_w3': nrm(ks[9], (DEPTH, 2, D_MODEL, D_FF), D_MODEL ** -0.5),
        'ffn_w2': nrm(ks[10], (DEPTH, 2, D_FF, D_MODEL), D_FF ** -0.5),
        'rg_w_in': nrm(ks[11], (N_RG_LAYERS, D_MODEL, 2 * D_RNN), D_MODEL ** -0.5),
        'rg_conv_w': nrm(ks[12], (N_RG_LAYERS, CONV_W, D_RNN), CONV_W ** -0.5),
        'rg_conv_b': nrm(ks[13], (N_RG_LAYERS, D_RNN), 0.02),
        'rg_w_a': nrm(ks[14], (N_RG_LAYERS, 2, RNN_HEADS, RNN_HEAD_DIM, RNN_HEAD_DIM), RNN_HEAD_DIM ** -0.5),
        'rg_b_a': nrm(ks[15], (N_RG_LAYERS, 2, D_RNN), 0.02),
        'rg_w_x': nrm(ks[16], (N_RG_LAYERS, 2, RNN_HEADS, RNN_HEAD_DIM, RNN_HEAD_DIM), RNN_HEAD_DIM ** -0.5),
        'rg_b_x': nrm(ks[17], (N_RG_LAYERS, 2, D_RNN), 0.02),
        'rg_lam': lam,
        'rg_w_out': nrm(ks[18], (N_RG_LAYERS, D_RNN, D_MODEL), D_RNN ** -0.5),
        'pool_w': nrm(ks[19], (N_POOL_LAYERS, POOL_GROUPS, POOL_GROUP_DIM, POOL_GROUP_DIM), POOL_GROUP_DIM ** -0.5),
        'pool_scale': 1.0 + nrm(ks[21], (N_POOL_LAYERS, D_MODEL), 0.1),
    }


def reference(x_prompt, x_sample, state_rglru, c, c_ctx, mod_w, mod_b, norm_g, ffn_w1, ffn_w3, ffn_w2,
              rg_w_in, rg_conv_w, rg_conv_b, rg_w_a, rg_b_a, rg_w_x, rg_b_x, rg_lam, rg_w_out,
              pool_w, pool_scale):
    rg_p = (rg_w_in, rg_conv_w, rg_conv_b, rg_w_a, rg_b_a, rg_w_x, rg_b_x, rg_lam, rg_w_out)
    pool_p = (pool_w, pool_scale)
    silu_ctx = jax.nn.silu(c_ctx)[None]
    silu_lat = jax.nn.silu(c)
    zeros = jnp.zeros((x_prompt.shape[0], D_RNN), x_prompt.dtype)
    xc, xs = x_prompt, x_sample
    ctx_states = []
    for l in range(DEPTH):
        mod_c = silu_ctx @ mod_w[l] + mod_b[l]
        mod_s = silu_lat @ mod_w[l] + mod_b[l]
        j = l // N_MIXERS
        if l % N_MIXERS == 0:
            mix_c = functools.partial(_rg_mixer, p=rg_p, j=j, h0_fwd=zeros, h0_bwd=zeros)
            mix_s = functools.partial(_rg_mixer, p=rg_p, j=j,
                                      h0_fwd=state_rglru[:, j, 0], h0_bwd=state_rglru[:, j, 1])
        else:
            mix_c = functools.partial(_pool_mixer, p=pool_p, j=j, on_grid=False)
            mix_s = functools.partial(_pool_mixer, p=pool_p, j=j, on_grid=True)
        xc, st = _layer(xc, mod_c, l, norm_g, ffn_w1, ffn_w3, ffn_w2, mix_c)
        if st is not None:
            ctx_states.append(st)
        xs, _ = _layer(xs, mod_s, l, norm_g, ffn_w1, ffn_w3, ffn_w2, mix_s)
    new_state_rglru = jnp.stack(ctx_states, axis=1)
    return (xc, xs, new_state_rglru)
```

```python
import numpy as np
import ml_dtypes
import concourse.bass as bass
import concourse.mybir as mybir
from concourse.bass_utils import run_bass_kernel_spmd
from concourse.ap import AP

F32 = mybir.dt.float32
BF16 = mybir.dt.bfloat16
AF = mybir.ActivationFunctionType
ALU = mybir.AluOpType

D = 1024
DFF = 2816
NT = 2048
NCT = 8
NJ = 22
DEPTH = 4
EPS = 1e-6
POOL_WINDOWS = (2, 4, 8, 16)
GRID_W = 64
N_CORES = 8
import os
DEBUG_STAGE = int(os.environ.get("DEBUG_STAGE", "0"))
RG_STAGE = int(os.environ.get("RG_STAGE", "0"))
FFN_CHAIN = int(os.environ.get("FFN_CHAIN", "1"))
Z_OVERLAP = int(os.environ.get("Z_OVERLAP", "1"))


def _small_cols():
    cols = {}
    off = 0
    for n, k in (("norm_g", 4 * 6 * 8), ("mod_b", 4 * 9 * 8), ("conv_w", 2 * 4 * 8), ("conv_b", 2 * 8),
                 ("b_a", 2 * 2 * 8), ("b_x", 2 * 2 * 8), ("lam", 2 * 2 * 8), ("pool_scale", 2 * 8),
                 ("h0", 2 * 2 * 8), ("cvec", 8), ("cross", 1)):
        cols[n] = off
        off += k
    return cols, off


SC, NSMALL = _small_cols()


def _fm(v):
    v = np.asarray(v, np.float32)
    return np.ascontiguousarray(v.reshape(-1, 8, 128).transpose(2, 0, 1)).reshape(128, -1)


def _ktiles(w, c8):
    r0 = 4 * c8
    lo = max(r0 - w // 2, 0)
    hi = min(r0 + 3 + (w - w // 2) - 1, 31)
    return list(range(lo // 2, hi // 2 + 1))


def _pool_index():
    idx = {}
    off = 0
    for g, w in enumerate(POOL_WINDOWS):
        for c8 in range(8):
            kts = _ktiles(w, c8)
            idx[(g, c8)] = (off, kts)
            off += len(kts)
    return idx, off


PIDX, PTOT = _pool_index()


def _win(n, w):
    t = np.arange(n)
    lo = np.maximum(t - w // 2, 0)
    hi = np.minimum(t + (w - w // 2), n)
    return lo, hi


def _pool_mats(on_grid):
    pall = np.zeros((128, PTOT, 256), np.float32)
    invc = np.zeros((4, NT), np.float32)
    for g, w in enumerate(POOL_WINDOWS):
        P = np.zeros((NT, NT), np.float32)
        if on_grid:
            rows = NT // GRID_W
            rlo, rhi = _win(rows, w)
            clo, chi = _win(GRID_W, w)
            R = np.zeros((rows, rows), np.float32)
            C = np.zeros((GRID_W, GRID_W), np.float32)
            for r in range(rows):
                R[r, rlo[r]:rhi[r]] = 1
            for c in range(GRID_W):
                C[c, clo[c]:chi[c]] = 1
            P = np.kron(R, C)
        else:
            lo, hi = _win(256, w)
            B = np.zeros((256, 256), np.float32)
            for t in range(256):
                B[t, lo[t]:hi[t]] = 1
            for s in range(8):
                P[s * 256:(s + 1) * 256, s * 256:(s + 1) * 256] = B
        cnt = P.sum(axis=1)
        invc[g] = 1.0 / cnt
        PT = P.T.copy()
        PT[np.arange(NT), np.arange(NT)] -= cnt
        for c8 in range(8):
            off, kts = PIDX[(g, c8)]
            cols = slice(c8 * 256, (c8 + 1) * 256)
            nz = np.nonzero(np.abs(PT[:, cols]).sum(axis=1))[0]
            assert all((t // 128) in kts for t in nz), (g, c8)
            for i, kt in enumerate(kts):
                pall[:, off + i, :] = PT[kt * 128:(kt + 1) * 128, cols]
    return pall.astype(ml_dtypes.bfloat16), np.ascontiguousarray(np.broadcast_to(invc[None], (128, 4, NT)))


import types


def _freeze(fn):
    if fn.__closure__ is None:
        return fn
    cells = []
    for c in fn.__closure__:
        try:
            cells.append(types.CellType(c.cell_contents))
        except ValueError:
            cells.append(c)
    return types.FunctionType(fn.__code__, fn.__globals__, fn.__name__, fn.__defaults__, tuple(cells))


class Res:
    __slots__ = ("w", "r")

    def __init__(self):
        self.w = None
        self.r = {}


class Tracker:
    ENGS = ("pe", "act", "dve", "pool", "sp")

    def __init__(self):
        self.ops = {e: [] for e in self.ENGS}
        self.count = {e: 0 for e in self.ENGS}
        self.semval = {}
        self.waited = {e: {} for e in self.ENGS}

    def _wait(self, eng, tok):
        key, val = tok
        if eng == "pe" and key == "pe":
            return
        if self.waited[eng].get(key, 0) >= val:
            return
        self.waited[eng][key] = val
        self.ops[eng].append(("wait", key, val))

    def _deps(self, eng, reads, writes):
        for r in reads:
            if r.w is not None:
                self._wait(eng, r.w)
        for w in writes:
            if w.w is not None:
                self._wait(eng, w.w)
            for k, v in w.r.items():
                self._wait(eng, (k, v))

    def _commit(self, tok, reads, writes):
        k, v = tok
        for r in reads:
            if r.r.get(k, 0) < v:
                r.r[k] = v
        for w in writes:
            w.w = tok
            w.r = {}

    def op(self, eng, fns, reads=(), writes=()):
        if callable(fns):
            fns = [fns]
        fns = [_freeze(f) for f in fns]
        self._deps(eng, reads, writes)
        self.count[eng] += 1
        tok = (eng, self.count[eng])
        for f in fns[:-1]:
            self.ops[eng].append(("op", f, False))
        self.ops[eng].append(("op", fns[-1], True))
        self._commit(tok, reads, writes)
        return tok

    def dma(self, queue, fns, sem, reads=(), writes=()):
        if callable(fns):
            fns = [fns]
        fns = [_freeze(f) for f in fns]
        self._deps(queue, reads, writes)
        self.semval[sem] = self.semval.get(sem, 0) + 16 * len(fns)
        tok = (sem, self.semval[sem])
        for fn in fns:
            self.ops[queue].append(("dma", fn, sem))
        self._commit(tok, reads, writes)
        return tok

    def fence(self, engs=("pe", "act", "dve", "sp")):
        for e in engs:
            for k in ("pe", "act", "dve"):
                if k != e and self.count[k] > 0:
                    self._wait(e, (k, self.count[k]))

    def final_wait(self, eng):
        for k, v in self.semval.items():
            self._wait(eng, (k, v))
        for k in ("pe", "act", "dve"):
            if self.count[k] > 0 and k != eng:
                self._wait(eng, (k, self.count[k]))


def build(plan=None, n_layers=DEPTH):
    if plan is None:
        plan = []
        for l in range(n_layers):
            nxt = l + 1 if l + 1 < n_layers else None
            if l == 0:
                plan.append(("mod", 0))
            plan += [("ffn", l, 0), ("rg", l, nxt) if l % 2 == 0 else ("pool", l, nxt), ("ffn", l, 1)]
    nc = bass.Bass("TRN2", target_bir_lowering=False)
    T = Tracker()

    def din(name, shape, dt=F32):
        return nc.dram_tensor(name, list(shape), dt, kind="ExternalInput").ap()

    x_in = din("x_in", [NT, D])
    small_in = din("small", [128, NSMALL])
    pall_in = din("pall", [128, PTOT, 256], BF16)
    invc_in = din("invc", [128, 4, NT])
    identf_in = din("identf", [128, 128])
    identb_in = din("identb", [128, 128], BF16)
    mod_w = din("mod_w", [4, D, 9 * D])
    ffn_w1 = din("ffn_w1", [4, 2, D, DFF])
    ffn_w3 = din("ffn_w3", [4, 2, D, DFF])
    ffn_w2 = din("ffn_w2", [4, 2, DFF, D])
    rg_w_in = din("rg_w_in", [2, D, 2 * D])
    rg_w_a = din("rg_w_a", [2, 2, 8, 128, 128])
    rg_w_x = din("rg_w_x", [2, 2, 8, 128, 128])
    rg_w_out = din("rg_w_out", [2, D, D])
    pool_w = din("pool_w", [2, 4, 256, 256])
    y_out = nc.dram_tensor("y_out", [NT, D], F32, kind="ExternalOutput").ap()
    st_out = nc.dram_tensor("st_out", [128, 256], F32, kind="ExternalOutput").ap()

    ARENA_E = 53312
    NSLOT = 2
    ctx = [
        nc.sbuf_tensor("sb_X", [128, NCT, NT], F32),
        nc.sbuf_tensor("sb_arena", [128, ARENA_E], BF16),
        nc.sbuf_tensor("sb_ring", [128, NSLOT, 4096], BF16),
        nc.sbuf_tensor("sb_small", [128, NSMALL], F32),
        nc.sbuf_tensor("sb_modt", [128, 2, 72], F32),
        nc.sbuf_tensor("sb_der", [128, 2, 72], F32),
        nc.sbuf_tensor("sb_cl", [128, 32], F32),
        nc.sbuf_tensor("sb_scb", [128, 8], BF16),
        nc.sbuf_tensor("sb_sctmp", [128, 8], F32),
        nc.sbuf_tensor("sb_ones", [128, 128], BF16),
        nc.sbuf_tensor("sb_identf", [128, 128], F32),
        nc.sbuf_tensor("sb_identb", [128, 128], BF16),
        nc.sbuf_tensor("sb_rstd", [128, NT], F32),
        nc.sbuf_tensor("sb_tmps", [128, 3, 512], F32),
        nc.sbuf_tensor("sb_states", [128, 256], F32),
    ]
    psum_ctx = [nc.psum_tensor(f"ps{i}", [128, 512], F32) for i in range(8)]
    sem_names = ["pe", "act", "dve", "pool_c", "ld_c", "ld_x0", "ld_x1", "ld_x2", "ld_x3", "slot0", "slot1",
                 "pl0", "pl1", "pl2", "pl3", "invc", "invc1", "st0", "st1", "st2", "st3", "stst", "modld", "ms0", "ms1", "ms2", "ms3", "ms4", "ms5"]
    sem_ctx = [nc.semaphore(n) for n in sem_names]

    import contextlib
    with contextlib.ExitStack() as es:
        (Xt, arena_t, ring_t, small_t, modt_t, der_t, cl_t, scb_t, sctmp_t, ones_t, identf_t, identb_t,
         rstd_t, tmps_t, states_t) = [es.enter_context(c) for c in ctx]
        banks = [es.enter_context(c) for c in psum_ctx]
        sems = {n: es.enter_context(c) for n, c in zip(sem_names, sem_ctx)}
        block = es.enter_context(nc.Block())

        X = Xt[:]
        arena = arena_t[:]
        small = small_t[:]
        M = {}
        R_mod2 = [Res(), Res()]
        R_der2 = [Res(), Res()]
        R_mb = Res()

        def set_layer(l):
            p = l % 2
            M["modt"] = modt_t[:, p, :]
            M["der"] = der_t[:, p, :]
            M["R_mod"] = R_mod2[p]
            M["R_der"] = R_der2[p]
        set_layer(0)
        cl = cl_t[:]
        rstd = rstd_t[:]
        tmps = tmps_t[:]
        states = states_t[:]
        ones = ones_t[:]

        R_X = [[Res() for _ in range(4)] for _ in range(NCT)]
        R_small = Res()
        R_cl = Res()
        R_scb = Res()
        R_ones = Res()
        R_ident = Res()
        R_rstd = Res()
        R_tmps = [Res() for _ in range(3)]
        R_states = Res()
        R_bank = [Res() for _ in range(8)]
        R_slot = [Res() for _ in range(NSLOT)]
        bank_rr = [0]
        slot_rr = [0]

        RR_BANKS = [0, 1, 2, 3, 4]
        R_st = [Res(), Res()]
        ring_rr = [0]

        def next_bank():
            rr = RR_BANKS if mod_state["job"] is not None else RR_BANKS + [7]
            b = rr[bank_rr[0] % len(rr)]
            bank_rr[0] += 1
            return banks[b][:], R_bank[b]

        def st_bank(c):
            return banks[5 + c][:], R_st[c]

        def next_slot(advance=True):
            s = slot_rr[0] % NSLOT
            if advance:
                slot_rr[0] += 1
            return ring_t[:, s, :], R_slot[s], f"slot{s}"

        def av(off_bytes, nelem, dt):
            assert off_bytes % 4 == 0
            if dt == BF16:
                assert off_bytes // 2 + nelem <= ARENA_E, (off_bytes, nelem)
                return arena[:, off_bytes // 2: off_bytes // 2 + nelem]
            assert off_bytes // 2 + 2 * nelem <= ARENA_E, (off_bytes, nelem)
            return arena[:, off_bytes // 2: off_bytes // 2 + 2 * nelem].bitcast(F32)

        KB = 1024

        def colS(name, i):
            o = SC[name] + i
            return small[:, o:o + 1]

        T.dma("sp", [lambda e: e.dma_start(out=small, in_=small_in[:, :]),
                     lambda e: e.dma_start(out=identf_t[:], in_=identf_in[:, :]),
                     lambda e: e.dma_start(out=identb_t[:], in_=identb_in[:, :])], "ld_c", writes=[R_small, R_ident])
        T.op("dve", lambda e: e.memset(ones, 1.0), writes=[R_ones])
        T.op("dve", lambda e: e.memset(states, 0.0), writes=[R_states])
        cv = small[:, SC["cvec"]:SC["cvec"] + 8]
        T.op("act", lambda e: e.activation(out=sctmp_t[:], in_=cv, func=AF.Silu), reads=[R_small], writes=[R_scb])
        T.op("dve", lambda e: e.tensor_copy(out=scb_t[:], in_=sctmp_t[:]), reads=[R_scb], writes=[R_scb])
        lamv = small[:, SC["lam"]:SC["lam"] + 32]
        c0, c1, c2, c3 = (tmps_t[:, 0, i * 32:(i + 1) * 32] for i in range(4))
        T.op("act", lambda e: e.activation(out=c0, in_=lamv, func=AF.Exp, scale=-1.0), reads=[R_small], writes=[R_cl])
        T.op("dve", lambda e: e.tensor_scalar_add(out=c1, in0=c0, scalar1=2.0), reads=[R_cl], writes=[R_cl])
        T.op("dve", lambda e: e.reciprocal(out=c1, in_=c1), reads=[R_cl], writes=[R_cl])
        T.op("dve", lambda e: e.tensor_mul(out=c1, in0=c0, in1=c1), reads=[R_cl], writes=[R_cl])
        T.op("dve", lambda e: e.tensor_mul(out=c2, in0=c1, in1=c1), reads=[R_cl], writes=[R_cl])
        T.op("dve", lambda e: e.tensor_scalar(out=c3, in0=c2, scalar1=1.0 / 11.0, scalar2=1.0 / 9.0,
                                              op0=ALU.mult, op1=ALU.add), reads=[R_cl], writes=[R_cl])
        for coef in (1.0 / 7.0, 1.0 / 5.0, 1.0 / 3.0, 1.0):
            T.op("dve", lambda e: e.tensor_mul(out=c3, in0=c3, in1=c2), reads=[R_cl], writes=[R_cl])
            T.op("dve", lambda e, coef=coef: e.tensor_scalar_add(out=c3, in0=c3, scalar1=coef),
                 reads=[R_cl], writes=[R_cl])
        T.op("dve", lambda e: e.tensor_mul(out=c3, in0=c3, in1=c1), reads=[R_cl], writes=[R_cl])
        T.op("dve", lambda e: e.tensor_scalar_mul(out=cl, in0=c3, scalar1=-16.0), reads=[R_cl], writes=[R_cl])

        def phase_a(pump_fn=None):
            STG = av(0, 4 * 1024, F32).rearrange("p (i n) -> p i n", i=4)
            R_stg = [Res() for _ in range(4)]
            for tq in range(4):
                for i in range(4):
                    tt = tq * 4 + i
                    T.dma("sp", lambda e, i=i, tt=tt: e.dma_start(out=STG[:, i, :], in_=x_in[tt * 128:(tt + 1) * 128, :]),
                          f"ld_x{i}", writes=[R_stg[i]])
                for ct in range(NCT):
                    bk, rb = next_bank()
                    fns = [lambda e, i=i, ct=ct, bk=bk: e.transpose(bk[:, i * 128:(i + 1) * 128],
                                                                     STG[:, i, ct * 128:(ct + 1) * 128], identf_t[:])
                           for i in range(4)]
                    T.op("pe", fns, reads=R_stg + [R_ident], writes=[rb])
                    eng = "act" if ct % 2 == 0 else "dve"
                    dst = X[:, ct, tq * 512:(tq + 1) * 512]
                    if eng == "act":
                        T.op("act", lambda e, dst=dst, bk=bk: e.activation(out=dst, in_=bk, func=AF.Copy),
                             reads=[rb], writes=[R_X[ct][tq]])
                    else:
                        T.op("dve", lambda e, dst=dst, bk=bk: e.tensor_copy(out=dst, in_=bk),
                             reads=[rb], writes=[R_X[ct][tq]])
                    if pump_fn is not None:
                        pump_fn()
            T.fence()

        def rstd_from_bank(stb, rst, dst_ap, r_dst, stage):
            if stage == 0:
                T.op("act", lambda e: e.activation(out=dst_ap, in_=stb, func=AF.Ln, scale=1.0 / D, bias=EPS),
                     reads=[rst], writes=[r_dst])
            else:
                T.op("act", lambda e: e.activation(out=dst_ap, in_=dst_ap, func=AF.Exp, scale=-0.5),
                     reads=[r_dst], writes=[r_dst])

        def stats_finish(c, dst_ap, r_dst):
            stb, rst = st_bank(c)
            rstd_from_bank(stb, rst, dst_ap, r_dst, 0)
            rstd_from_bank(stb, rst, dst_ap, r_dst, 1)

        def x_stats_ring(chunks, SQR, r_sqr):
            used = []
            for c in chunks:
                stb, rst = next_bank()
                used.append((c, stb, rst))
                for ct in range(NCT):
                    i = ring_rr[0] % 4
                    ring_rr[0] += 1
                    src = X[:, ct, c * 512:(c + 1) * 512]
                    if ct % 2 == 0:
                        T.op("act", lambda e, i=i, src=src: e.activation(out=SQR[:, i, :], in_=src, func=AF.Square),
                             reads=[R_X[ct][c]], writes=[r_sqr[i]])
                    else:
                        T.op("dve", lambda e, i=i, src=src: e.tensor_mul(out=SQR[:, i, :], in0=src, in1=src),
                             reads=[R_X[ct][c]], writes=[r_sqr[i]])
                    T.op("pe", lambda e, i=i, ct=ct, stb=stb: e.matmul(stb, ones, SQR[:, i, :], start=(ct == 0),
                                                                       stop=(ct == NCT - 1)),
                         reads=[r_sqr[i], R_ones], writes=[rst])
            for stage in range(2):
                for c, stb, rst in used:
                    rstd_from_bank(stb, rst, rstd[:, c * 512:(c + 1) * 512], R_rstd, stage)

        def inherit(news, olds):
            for n in news:
                for o in olds:
                    toks = dict(o.r)
                    if o.w is not None:
                        toks[o.w[0]] = max(toks.get(o.w[0], 0), o.w[1])
                    for k, v in toks.items():
                        if n.r.get(k, 0) < v:
                            n.r[k] = v

        def modulate(chunks, Hout, r_h, sub):
            for ci, c in enumerate(chunks):
                for ct in range(NCT):
                    k = (ci * NCT + ct) % 2
                    tmp = tmps[:, k, :]
                    g_ap = M["der"][:, sub * 24 + ct: sub * 24 + ct + 1]
                    sh_ap = M["modt"][:, sub * 24 + ct: sub * 24 + ct + 1]
                    T.op("dve", lambda e, tmp=tmp, ct=ct, c=c, g_ap=g_ap: e.scalar_tensor_tensor(
                        out=tmp, in0=X[:, ct, c * 512:(c + 1) * 512], scalar=g_ap, in1=rstd[:, c * 512:(c + 1) * 512],
                        op0=ALU.mult, op1=ALU.mult), reads=[R_X[ct][c], M["R_der"], R_rstd], writes=[R_tmps[k]])
                    T.op("act", lambda e, tmp=tmp, ct=ct, ci=ci, sh_ap=sh_ap: e.activation(
                        out=Hout[:, ct, ci * 512:(ci + 1) * 512], in_=tmp, func=AF.Identity, bias=sh_ap, scale=1.0),
                        reads=[R_tmps[k], M["R_mod"]], writes=[r_h[ct][c]])

        def out_proj(tg, mm_emit, Y, r_y, SQR, r_sqr, RSTD2, r_rstd2, sub, pool_scale_cols=None, invc=None, r_invc=None,
                     defer=None, drip_in=None):
            pending = []

            def flush(keep):
                while len(pending) > keep:
                    c, m, i = pending.pop(0)
                    stb, rst = st_bank(c)
                    T.op("pe", lambda e, i=i, m=m, stb=stb: e.matmul(stb, ones, SQR[:, i, :], start=(m == 0),
                                                                     stop=(m == NCT - 1)),
                         reads=[r_sqr[i], R_ones], writes=[rst])

            for m in range(NCT):
                for c in range(2):
                    bk, rb = next_bank()
                    fns, rr = mm_emit(m, c, bk)
                    T.op("pe", fns, reads=rr, writes=[rb])
                    ydst = Y[:, m, c * 512:(c + 1) * 512]
                    i = ring_rr[0] % 4
                    ring_rr[0] += 1
                    if pool_scale_cols is None:
                        T.op("dve", lambda e, ydst=ydst, bk=bk: e.tensor_copy(out=ydst, in_=bk),
                             reads=[rb], writes=[r_y[m]])
                    else:
                        g = m // 2
                        psc = pool_scale_cols[m]
                        iv = invc[:, g, c * 512:(c + 1) * 512]
                        T.op("dve", lambda e, ydst=ydst, bk=bk, psc=psc, iv=iv: e.scalar_tensor_tensor(
                            out=ydst, in0=bk, scalar=psc, in1=iv, op0=ALU.mult, op1=ALU.mult),
                            reads=[rb, R_small, r_invc], writes=[r_y[m]])
                    T.op("act", lambda e, i=i, ydst=ydst: e.activation(out=SQR[:, i, :], in_=ydst, func=AF.Square),
                         reads=[r_y[m]], writes=[r_sqr[i]])
                    pending.append((c, m, i))
                    flush(2)
                    if drip_in:
                        drip_in.pop(0)()
            flush(0)
            while drip_in:
                drip_in.pop(0)()
            tail = []
            for stage in range(2):
                for c in range(2):
                    tail.append(lambda c=c, stage=stage: rstd_from_bank(*st_bank(c), RSTD2[:, c * 512:(c + 1) * 512],
                                                                        r_rstd2, stage))
            r_der_now = M["R_der"]
            for m in range(NCT):
                gg = M["der"][:, sub * 24 + 16 + m: sub * 24 + 16 + m + 1]
                xs = X[:, m, tg * 1024:(tg + 1) * 1024]

                def t1(m=m, gg=gg):
                    T.op("dve", lambda e, m=m, gg=gg: e.scalar_tensor_tensor(
                        out=Y[:, m, :], in0=Y[:, m, :], scalar=gg, in1=RSTD2, op0=ALU.mult, op1=ALU.mult),
                        reads=[r_y[m], r_der_now, r_rstd2], writes=[r_y[m]])

                def t2(m=m, xs=xs):
                    T.op("dve", lambda e, m=m, xs=xs: e.tensor_add(out=xs, in0=xs, in1=Y[:, m, :]),
                         reads=[r_y[m], R_X[m][2 * tg], R_X[m][2 * tg + 1]],
                         writes=[R_X[m][2 * tg], R_X[m][2 * tg + 1]])
                tail += [t1, t2]
            if defer is None:
                for t in tail:
                    t()
            else:
                defer.extend(tail)

        mod_state = {"job": None, "advance": True}

        def mod_job(l):
            p = l % 2
            modt = modt_t[:, p, :]
            der = der_t[:, p, :]
            r_mod, r_der = R_mod2[p], R_der2[p]
            bk, rb = banks[7][:], R_bank[7]
            wv = mod_w[l].rearrange("(k p) n -> p k n", p=128)
            for blk in range(18):
                if mod_state.get("extra"):
                    sl, rs, sname = mod_state["extra"][blk % len(mod_state["extra"])]
                else:
                    sl, rs, sname = next_slot(advance=mod_state["advance"])
                sv = sl.rearrange("p (k n) -> p k n", k=8)
                T.dma("pool", lambda e, sv=sv, blk=blk: e.dma_start(out=sv, in_=wv[:, :, blk * 512:(blk + 1) * 512]),
                      sname, writes=[rs])
                fns = []
                for mi in range(4):
                    col = blk * 4 + mi
                    for k in range(8):
                        fns.append(lambda e, sv=sv, mi=mi, k=k, col=col: e.matmul(
                            bk[:, col:col + 1], sv[:, k, mi * 128:(mi + 1) * 128], scb_t[:, k:k + 1],
                            start=(k == 0), stop=(k == 7)))
                T.op("pe", fns, reads=[rs, R_scb], writes=[rb])
                yield
            mb = small[:, SC["mod_b"] + l * 72: SC["mod_b"] + (l + 1) * 72]
            T.op("dve", lambda e: e.tensor_tensor(out=modt, in0=bk[:, 0:72], in1=mb, op=ALU.add),
                 reads=[rb, R_small], writes=[r_mod])
            for s in range(3):
                ng_a = small[:, SC["norm_g"] + (l * 6 + 2 * s) * 8: SC["norm_g"] + (l * 6 + 2 * s) * 8 + 8]
                ng_b = small[:, SC["norm_g"] + (l * 6 + 2 * s + 1) * 8: SC["norm_g"] + (l * 6 + 2 * s + 1) * 8 + 8]
                T.op("dve", lambda e, s=s, ng_a=ng_a: e.scalar_tensor_tensor(
                    out=der[:, s * 24:s * 24 + 8], in0=modt[:, s * 24 + 8:s * 24 + 16], scalar=1.0, in1=ng_a,
                    op0=ALU.add, op1=ALU.mult), reads=[r_mod, R_small], writes=[r_der])
                T.op("dve", lambda e, s=s, ng_b=ng_b: e.scalar_tensor_tensor(
                    out=der[:, s * 24 + 16:s * 24 + 24], in0=modt[:, s * 24 + 16:s * 24 + 24],
                    scalar=(1.0 if s == 1 else 0.5), in1=ng_b, op0=ALU.mult, op1=ALU.mult),
                    reads=[r_mod, R_small], writes=[r_der])

        def pump(n=1):
            j = mod_state["job"]
            for _ in range(n):
                if j is None:
                    return
                try:
                    next(j)
                except StopIteration:
                    mod_state["job"] = j = None

        def phase_mod(l):
            mod_state["advance"] = True
            mod_state["job"] = mod_job(l)
            pump(1000)

        R_HB = Res()
        FR = {"g": [Res() for _ in range(NJ)], "y": [Res() for _ in range(NCT)], "sa": [Res(), Res()],
              "sqr": [Res() for _ in range(4)], "rstd2": Res()}

        def phase_ffn(l, f, pre_done=False, pre_next=None, zdrip=False):
            set_layer(l)
            sub = 0 if f == 0 else 2
            HB = av(0, 8 * 1024, BF16).rearrange("p (c n) -> p c n", c=8)
            G = av(16 * KB, NJ * 1024, BF16).rearrange("p (j n) -> p j n", j=NJ)
            Y = av(60 * KB, 8 * 1024, F32).rearrange("p (c n) -> p c n", c=8)
            SA = av(92 * KB, 2 * 512, F32).rearrange("p (i n) -> p i n", i=2)
            SQR = av(96 * KB, 4 * 512, BF16).rearrange("p (i n) -> p i n", i=4)
            RSTD2 = av(100 * KB, 1024, F32)
            r_hb = R_HB
            r_g, r_y, r_sa, r_sqr, r_rstd2 = FR["g"], FR["y"], FR["sa"], FR["sqr"], FR["rstd2"]
            w1v = ffn_w1[l, f].rearrange("(k p) n -> p k n", p=128)
            w3v = ffn_w3[l, f].rearrange("(k p) n -> p k n", p=128)
            w2v = ffn_w2[l, f].rearrange("(j p) n -> p j n", p=128)

            def emit_h(tg, sub=sub):
                for ci in range(2):
                    c = 2 * tg + ci
                    for ct in range(NCT):
                        k = (ci * NCT + ct) % 2
                        tmp = tmps[:, k, :]
                        g_ap = M["der"][:, sub * 24 + ct: sub * 24 + ct + 1]
                        sh_ap = M["modt"][:, sub * 24 + ct: sub * 24 + ct + 1]
                        T.op("dve", lambda e, tmp=tmp, ct=ct, c=c, g_ap=g_ap: e.scalar_tensor_tensor(
                            out=tmp, in0=X[:, ct, c * 512:(c + 1) * 512], scalar=g_ap,
                            in1=rstd[:, c * 512:(c + 1) * 512], op0=ALU.mult, op1=ALU.mult),
                            reads=[R_X[ct][c], M["R_der"], R_rstd], writes=[R_tmps[k]])
                        T.op("act", lambda e, tmp=tmp, ct=ct, ci=ci, sh_ap=sh_ap: e.activation(
                            out=HB[:, ct, ci * 512:(ci + 1) * 512], in_=tmp, func=AF.Identity, bias=sh_ap, scale=1.0),
                            reads=[R_tmps[k], M["R_mod"]], writes=[r_hb])

            if pre_done:
                pass
            else:
                x_stats_ring([0, 1, 2, 3], SQR, r_sqr)
                emit_h(0)
            drip = []
            for tg in range(2):
                if DEBUG_STAGE == 1:
                    continue
                for jb in range(NJ // 2):
                    sl, rs, sname = next_slot()
                    s1 = sl[:, 0:2048].rearrange("p (k n) -> p k n", k=8)
                    s3 = sl[:, 2048:4096].rearrange("p (k n) -> p k n", k=8)
                    T.dma("pool", [lambda e, s1=s1, jb=jb: e.dma_start(out=s1, in_=w1v[:, :, jb * 256:(jb + 1) * 256]),
                                   lambda e, s3=s3, jb=jb: e.dma_start(out=s3, in_=w3v[:, :, jb * 256:(jb + 1) * 256])],
                          sname, writes=[rs])
                    for jj in range(2):
                        j = jb * 2 + jj
                        for c in range(2):
                            bka, rba = next_bank()
                            bkb, rbb = next_bank()
                            fa = [lambda e, k=k, jj=jj, c=c, bka=bka, s1=s1: e.matmul(
                                bka, s1[:, k, jj * 128:(jj + 1) * 128], HB[:, k, c * 512:(c + 1) * 512],
                                start=(k == 0), stop=(k == 7)) for k in range(8)]
                            T.op("pe", fa, reads=[rs, r_hb], writes=[rba])
                            fb = [lambda e, k=k, jj=jj, c=c, bkb=bkb, s3=s3: e.matmul(
                                bkb, s3[:, k, jj * 128:(jj + 1) * 128], HB[:, k, c * 512:(c + 1) * 512],
                                start=(k == 0), stop=(k == 7)) for k in range(8)]
                            T.op("pe", fb, reads=[rs, r_hb], writes=[rbb])
                            si = (j * 2 + c) % 2
                            T.op("act", lambda e, si=si, bka=bka: e.activation(out=SA[:, si, :], in_=bka, func=AF.Silu),
                                 reads=[rba], writes=[r_sa[si]])
                            T.op("dve", lambda e, si=si, bkb=bkb, j=j, c=c: e.tensor_tensor(
                                out=G[:, j, c * 512:(c + 1) * 512], in0=SA[:, si, :], in1=bkb, op=ALU.mult),
                                reads=[r_sa[si], rbb], writes=[r_g[j]])
                            if drip:
                                drip.pop(0)()
                            if pre_done and tg == 0 and jb == 2 and jj == 1 and c == 1:
                                x_stats_ring([2, 3], SQR, r_sqr)
                if DEBUG_STAGE == 2:
                    continue
                if tg == 0:
                    emit_h(1)
                slots = {}

                def mm_down(m, c, bk):
                    if c == 0:
                        sl, rs, sname = next_slot()
                        s2 = sl[:, 0:NJ * 128].rearrange("p (j n) -> p j n", j=NJ)
                        T.dma("pool", [lambda e, s2=s2, m=m, q=q: e.dma_start(
                            out=s2[:, q * 11:(q + 1) * 11, :], in_=w2v[:, q * 11:(q + 1) * 11, m * 128:(m + 1) * 128])
                            for q in range(2)], sname, writes=[rs])
                        slots[m] = (s2, rs)
                    s2, rs = slots[m]
                    fns = [lambda e, j=j, c=c, bk=bk, s2=s2: e.matmul(bk, s2[:, j, :], G[:, j, c * 512:(c + 1) * 512],
                                                                      start=(j == 0), stop=(j == NJ - 1))
                           for j in range(NJ)]
                    return fns, [rs] + r_g

                while drip:
                    drip.pop(0)()
                if tg == 1 and pre_next is not None:
                    set_layer(pre_next)
                    x_stats_ring([0, 1], SQR, r_sqr)
                    emit_h(0, sub=0)
                    set_layer(l)
                zl = None
                if tg == 1 and zdrip:
                    inherit(r_ost, [R_HB])
                    zl = [(lambda tt=tt, half=half: out_unit(tt, half)) for tt in range(8) for half in range(2)]
                out_proj(tg, mm_down, Y, r_y, SQR, r_sqr, RSTD2, r_rstd2, sub, defer=(drip if tg == 0 else None),
                         drip_in=zl)
            if pre_next is None:
                T.fence()

        def mixer_h(HM, r_hm):
            SQR = av(32 * KB, 4 * 512, BF16).rearrange("p (i n) -> p i n", i=4)
            r_sqr = [Res() for _ in range(4)]
            x_stats_ring([0, 1, 2, 3], SQR, r_sqr)
            modulate([0, 1, 2, 3], HM, r_hm, 1)
            return r_sqr

        def phase_rg(l, nxt=None):
            set_layer(l)
            if nxt is not None:
                mod_state["advance"] = False
                mod_state["job"] = mod_job(nxt)
            j = l // 2
            HM = av(0, 8 * NT, BF16).rearrange("p (c n) -> p c n", c=8)
            r_hm = [[Res() for _ in range(4)] for _ in range(NCT)]
            r_sqr_mh = mixer_h(HM, r_hm)
            YB = av(32 * KB, 8 * NT, BF16).rearrange("p (c n) -> p c n", c=8)
            AAB = av(64 * KB, NT, F32)
            XRP = av(72 * KB, 8 * 260, F32).rearrange("p (s n) -> p s n", s=8)
            UB = av(72 * KB, NT, F32)
            XC = av(72 * KB + 8320, NT, F32)
            AAF = av(72 * KB + 8320 + 8 * KB, NT, F32)
            UF = av(72 * KB + 8320 + 16 * KB, NT, F32)
            XCB = rstd_t[:, 0:1024].bitcast(BF16)
            T2A = tmps_t[:, 0:2, :].rearrange("p a n -> p (a n)")
            T2B = rstd_t[:, 1024:2048]
            r_yb = [Res() for _ in range(NCT)]
            r_xrp, r_xc, r_aaf, r_aab, r_uf = Res(), Res(), Res(), Res(), Res()
            r_ub = r_xrp
            r_xcb = [Res() for _ in range(4)]
            r_t2 = [Res(), Res()]
            TT = [T2A, T2B]
            inherit(r_yb, r_sqr_mh)
            inherit(r_xcb + [r_t2[1]], [R_rstd])
            inherit([r_t2[0]], [R_tmps[0], R_tmps[1]])
            cross = colS("cross", 0)
            wv = rg_w_in[j].rearrange("(k p) n -> p k n", p=128)
            T.op("dve", lambda e: e.memset(XRP, 0.0), writes=[r_xrp])
            XC3 = XC.rearrange("p (s n) -> p s n", s=8)
            pending_yb = []
            for h in range(8):
                sl, rs, sname = next_slot()
                sg = sl[:, 0:1024].rearrange("p (k n) -> p k n", k=8)
                sx = sl[:, 1024:2048].rearrange("p (k n) -> p k n", k=8)
                dfn = [lambda e, sg=sg, h=h: e.dma_start(out=sg, in_=wv[:, :, h * 128:(h + 1) * 128]),
                       lambda e, sx=sx, h=h: e.dma_start(out=sx, in_=wv[:, :, D + h * 128:D + (h + 1) * 128])]
                gw = {}
                for d in range(2):
                    for gi, wsrc in enumerate((rg_w_a, rg_w_x)):
                        o = 2048 + (d * 2 + gi) * 128
                        gw[(d, gi)] = sl[:, o:o + 128]
                        dfn.append(lambda e, o=o, d=d, h=h, wsrc=wsrc, sl=sl: e.dma_start(
                            out=sl[:, o:o + 128], in_=wsrc[j, d, h]))
                T.dma("pool", dfn, sname, writes=[rs])
                for c in range(4):
                    bk2, rb2 = next_bank()
                    fns = [lambda e, k=k, c=c, bk2=bk2, sx=sx: e.matmul(bk2, sx[:, k, :], HM[:, k, c * 512:(c + 1) * 512],
                                                                        start=(k == 0), stop=(k == 7)) for k in range(8)]
                    T.op("pe", fns, reads=[rs] + [r_hm[k][c] for k in range(NCT)], writes=[rb2])
                    T.op("dve", lambda e, c=c, bk2=bk2: e.tensor_copy(
                        out=XRP[:, 2 * c:2 * c + 2, 1:257], in_=bk2.rearrange("p (s n) -> p s n", s=2)),
                        reads=[rb2], writes=[r_xrp])
                    if c in (1, 3):
                        pump(1)
                T.op("dve", lambda e: e.tensor_scalar_mul(out=XRP[:, 1:8, 0:1], in0=XRP[:, 0:7, 256:257], scalar1=cross),
                     reads=[r_xrp, R_small], writes=[r_xrp])
                T.op("dve", lambda e: e.tensor_scalar_mul(out=XRP[:, 0:7, 257:259], in0=XRP[:, 1:8, 1:3], scalar1=cross),
                     reads=[r_xrp, R_small], writes=[r_xrp])
                T.op("dve", lambda e: e.memset(XRP[:, 0:1, 0:1], 0.0), writes=[r_xrp])
                T.op("dve", lambda e: e.memset(XRP[:, 7:8, 257:260], 0.0), writes=[r_xrp])
                cw = [colS("conv_w", (j * 4 + k) * 8 + h) for k in range(4)]
                cb = colS("conv_b", j * 8 + h)
                T.op("dve", lambda e: e.tensor_scalar(out=XC3, in0=XRP[:, :, 0:256], scalar1=cw[0], scalar2=cb,
                                                      op0=ALU.mult, op1=ALU.add), reads=[r_xrp, R_small], writes=[r_xc])
                for k in range(1, 4):
                    T.op("dve", lambda e, k=k: e.scalar_tensor_tensor(out=XC3, in0=XRP[:, :, k:k + 256], scalar=cw[k],
                                                                      in1=XC3, op0=ALU.mult, op1=ALU.add),
                         reads=[r_xrp, r_xc, R_small], writes=[r_xc])
                for c in range(4):
                    cs = slice(c * 512, (c + 1) * 512)
                    T.op("act", lambda e, cs=cs: e.activation(out=XCB[:, cs], in_=XC[:, cs], func=AF.Copy),
                         reads=[r_xc], writes=[r_xcb[c]])
                while pending_yb:
                    pending_yb.pop(0)()

                def gates(d):
                    AA, r_a = (AAF, r_aaf) if d == 0 else (AAB, r_aab)
                    U, r_u = (UF, r_uf) if d == 0 else (UB, r_ub)
                    ba = colS("b_a", (j * 2 + d) * 8 + h)
                    bx = colS("b_x", (j * 2 + d) * 8 + h)
                    clc = cl[:, (j * 2 + d) * 8 + h:(j * 2 + d) * 8 + h + 1]
                    for c in range(4):
                        cs = slice(c * 512, (c + 1) * 512)
                        bkr, rbr = next_bank()
                        T.op("pe", lambda e, bkr=bkr, cs=cs: e.matmul(bkr, gw[(d, 0)], XCB[:, cs], start=True, stop=True),
                             reads=[rs, r_xcb[c]], writes=[rbr])
                        bki, rbi = next_bank()
                        T.op("pe", lambda e, bki=bki, cs=cs: e.matmul(bki, gw[(d, 1)], XCB[:, cs], start=True, stop=True),
                             reads=[rs, r_xcb[c]], writes=[rbi])
                        T.op("act", lambda e, bkr=bkr, cs=cs: e.activation(out=AA[:, cs], in_=bkr, func=AF.Sigmoid, bias=ba, scale=1.0),
                             reads=[rbr, R_small], writes=[r_a])
                        T.op("act", lambda e, bki=bki, cs=cs: e.activation(out=U[:, cs], in_=bki, func=AF.Sigmoid, bias=bx, scale=1.0),
                             reads=[rbi, R_small], writes=[r_u])
                    T.op("act", lambda e: e.activation(out=AA, in_=AA, func=AF.Exp, scale=clc), reads=[r_a, R_cl], writes=[r_a])

                def a2(d, half):
                    AA, r_a = (AAF, r_aaf) if d == 0 else (AAB, r_aab)
                    hs = slice(half * 1024, (half + 1) * 1024)
                    Tb, rt = TT[half], r_t2[half]
                    T.op("dve", lambda e: e.tensor_mul(out=Tb, in0=AA[:, hs], in1=AA[:, hs]), reads=[r_a], writes=[rt])

                def sq(half):
                    Tb, rt = TT[half], r_t2[half]
                    T.op("act", lambda e: e.activation(out=Tb, in_=Tb, func=AF.Sqrt, scale=-1.0, bias=1.0),
                         reads=[rt], writes=[rt])

                def muls(d, half):
                    U, r_u = (UF, r_uf) if d == 0 else (UB, r_ub)
                    hs = slice(half * 1024, (half + 1) * 1024)
                    Tb, rt = TT[half], r_t2[half]
                    T.op("dve", lambda e: e.tensor_mul(out=U[:, hs], in0=U[:, hs], in1=Tb), reads=[r_u, rt], writes=[r_u])
                    T.op("dve", lambda e: e.tensor_mul(out=U[:, hs], in0=U[:, hs], in1=XC[:, hs]), reads=[r_u, r_xc], writes=[r_u])

                def scan(d):
                    AA, r_a = (AAF, r_aaf) if d == 0 else (AAB, r_aab)
                    U, r_u = (UF, r_uf) if d == 0 else (UB, r_ub)
                    h0c = colS("h0", (j * 2 + d) * 8 + h)
                    if d == 0:
                        ucol, acol, amask = U[:, 0:1], AA[:, 0:1], AA[:, 256::256]
                    else:
                        ucol, acol, amask = U[:, NT - 1:NT], AA[:, NT - 1:NT], AA[:, 255:1792:256]
                    T.op("dve", lambda e: e.scalar_tensor_tensor(out=ucol, in0=acol, scalar=h0c, in1=ucol,
                                                                 op0=ALU.mult, op1=ALU.add),
                         reads=[r_a, r_u, R_small], writes=[r_u])
                    T.op("dve", lambda e: e.tensor_scalar_mul(out=amask, in0=amask, scalar1=cross),
                         reads=[r_a, R_small], writes=[r_a])
                    if d == 0:
                        T.op("dve", lambda e: e.tensor_tensor_scan(out=U, data0=AA, data1=U, initial=0.0,
                                                                   op0=ALU.mult, op1=ALU.add), reads=[r_a, r_u], writes=[r_u])
                        fin = U[:, 255::256]
                    else:
                        T.op("dve", lambda e: e.tensor_tensor_scan(out=U[:, ::-1], data0=AA[:, ::-1], data1=U[:, ::-1],
                                                                   initial=0.0, op0=ALU.mult, op1=ALU.add),
                             reads=[r_a, r_u], writes=[r_u])
                        fin = U[:, 0::256]
                    so = ((j * 2 + d) * 8 + h) * 8
                    T.op("act", lambda e: e.activation(out=states[:, so:so + 8], in_=fin, func=AF.Copy),
                         reads=[r_u], writes=[R_states])

                gates(0)
                pump(1)
                a2(0, 0); a2(0, 1); sq(0); sq(1)
                gates(1)
                muls(0, 0); muls(0, 1); scan(0)
                a2(1, 0); a2(1, 1); sq(0); sq(1)
                muls(1, 0); muls(1, 1); scan(1)
                T.op("dve", lambda e: e.tensor_add(out=UF, in0=UF, in1=UB), reads=[r_uf, r_ub], writes=[r_uf])
                for c in range(4):
                    cs = slice(c * 512, (c + 1) * 512)
                    bk, rb = next_bank()
                    fns = [lambda e, k=k, c=c, bk=bk, sg=sg: e.matmul(bk, sg[:, k, :], HM[:, k, c * 512:(c + 1) * 512],
                                                                      start=(k == 0), stop=(k == 7)) for k in range(8)]
                    T.op("pe", fns, reads=[rs] + [r_hm[k][c] for k in range(NCT)], writes=[rb])
                    Gt = TT[c // 2][:, (c % 2) * 512:(c % 2 + 1) * 512]
                    rt = r_t2[c // 2]
                    T.op("act", lambda e, bk=bk, Gt=Gt: e.activation(out=Gt, in_=bk, func=AF.Gelu_apprx_tanh),
                         reads=[rb], writes=[rt])
                    pending_yb.append(lambda cs=cs, Gt=Gt, h=h, rt=rt: T.op(
                        "dve", lambda e: e.tensor_mul(out=YB[:, h, cs], in0=UF[:, cs], in1=Gt),
                        reads=[r_uf, rt], writes=[r_yb[h]]))
            while pending_yb:
                pending_yb.pop(0)()
            pump(1000)
            T.fence()
            if RG_STAGE and RG_STAGE < 7:
                return
            wo = rg_w_out[j].rearrange("(k p) n -> p k n", p=128)
            wsl = []
            for half in range(2):
                sl, rs, sname = next_slot()
                sv = sl.rearrange("p (k n) -> p k n", k=8)
                T.dma("pool", lambda e, sv=sv, half=half: e.dma_start(out=sv, in_=wo[:, :, half * 512:(half + 1) * 512]),
                      sname, writes=[rs])
                wsl.append((sv, rs))
            Ys = [av(0, 8 * 1024, F32).rearrange("p (c n) -> p c n", c=8),
                  av(64 * KB, 8 * 1024, F32).rearrange("p (c n) -> p c n", c=8)]
            SQR = av(96 * KB, 4 * 512, BF16).rearrange("p (i n) -> p i n", i=4)
            RSTD2 = av(100 * KB, 1024, F32)
            r_ys = [[Res() for _ in range(NCT)] for _ in range(2)]
            r_sqr = [Res() for _ in range(4)]
            r_rstd2s = [Res(), Res()]
            RSTD2s = [av(100 * KB, 1024, F32), rstd_t[:, 0:1024]]
            mix_drip = []
            for tg in range(2):
                def mm_out(m, c, bk, tg=tg):
                    sv, rs = wsl[m // 4]
                    mo = (m % 4) * 128
                    t0 = tg * 1024 + c * 512
                    fns = [lambda e, k=k, bk=bk, sv=sv, mo=mo, t0=t0: e.matmul(bk, sv[:, k, mo:mo + 128], YB[:, k, t0:t0 + 512],
                                                                               start=(k == 0), stop=(k == 7))
                           for k in range(8)]
                    return fns, [rs] + r_yb
                if RG_STAGE == 7 and tg == 1:
                    continue
                out_proj(tg, mm_out, Ys[tg], r_ys[tg], SQR, r_sqr, RSTD2s[tg], r_rstd2s[tg], 1,
                         defer=(mix_drip if tg == 0 else None), drip_in=(mix_drip if tg == 1 else None))
            T.fence()

        def phase_pool(l, nxt=None):
            set_layer(l)
            if nxt is not None:
                mod_state["advance"] = True
                mod_state["job"] = mod_job(nxt)
            j = l // 2
            HM = av(0, 8 * NT, BF16).rearrange("p (c n) -> p c n", c=8)
            r_hm = [[Res() for _ in range(4)] for _ in range(NCT)]
            r_sqr_mh = mixer_h(HM, r_hm)
            INVCs = [av(84 * KB, 4 * 1024, F32).rearrange("p (g n) -> p g n", g=4),
                     av(64 * KB, 4 * 1024, F32).rearrange("p (g n) -> p g n", g=4)]
            r_invcs = [Res(), Res()]
            T.dma("sp", lambda e: e.dma_start(out=INVCs[0], in_=invc_in[:, :, 0:1024]), "invc", writes=[r_invcs[0]])
            HT = av(32 * KB, 16 * 1024, BF16).rearrange("p (t n) -> p t n", t=16)
            r_ht = [Res() for _ in range(16)]
            inherit(r_ht, r_sqr_mh)
            for tt in range(16):
                bk, rb = next_bank()
                bkb = bk.bitcast(BF16)
                fns = [lambda e, ct=ct, tt=tt, bkb=bkb: e.transpose(bkb[:, ct * 128:(ct + 1) * 128],
                                                                     HM[:, ct, tt * 128:(tt + 1) * 128], identb_t[:])
                       for ct in range(NCT)]
                T.op("pe", fns, reads=[r_hm[ct][tt // 4] for ct in range(NCT)] + [R_ident], writes=[rb])
                if tt % 2 == 0:
                    T.op("act", lambda e, tt=tt, bkb=bkb: e.activation(out=HT[:, tt, :], in_=bkb, func=AF.Copy),
                         reads=[rb], writes=[r_ht[tt]])
                else:
                    T.op("dve", lambda e, tt=tt, bkb=bkb: e.tensor_copy(out=HT[:, tt, :], in_=bkb),
                         reads=[rb], writes=[r_ht[tt]])
            DB = HM
            r_db = r_hm
            NPS = 4
            PS = av(64 * KB, NPS * 10 * 256, BF16).rearrange("p (s k n) -> p s k n", s=NPS, k=10)
            r_ps = [Res() for _ in range(NPS)]
            pi = 0
            for g in range(4):
                for c8 in range(8):
                    off, kts = PIDX[(g, c8)]
                    nk = len(kts)
                    s = pi % NPS
                    pi += 1
                    T.dma("sp", lambda e, s=s, off=off, nk=nk: e.dma_start(out=PS[:, s, 0:nk, :], in_=pall_in[:, off:off + nk, :]),
                          f"pl{s}", writes=[r_ps[s]])
                    for ct2 in range(2):
                        ct = 2 * g + ct2
                        bk, rb = next_bank()
                        fns = [lambda e, i=i, kt=kt, ct=ct, s=s, bk=bk: e.matmul(
                            bk[:, 0:256], HT[:, kt, ct * 128:(ct + 1) * 128], PS[:, s, i, :],
                            start=(i == 0), stop=(i == nk - 1)) for i, kt in enumerate(kts)]
                        T.op("pe", fns, reads=[r_ps[s]] + [r_ht[kt] for kt in kts], writes=[rb])
                        dst = DB[:, ct, c8 * 256:(c8 + 1) * 256]
                        if ct2 == 0:
                            T.op("act", lambda e, dst=dst, bk=bk: e.activation(out=dst, in_=bk[:, 0:256], func=AF.Copy),
                                 reads=[rb], writes=[r_db[ct][c8 // 2]])
                        else:
                            T.op("dve", lambda e, dst=dst, bk=bk: e.tensor_copy(out=dst, in_=bk[:, 0:256]),
                                 reads=[rb], writes=[r_db[ct][c8 // 2]])
                        if ct2 == 1:
                            pump(1)
            pump(1000)
            T.fence()
            sl, rs, sname = next_slot()
            pw = sl[:, 0:2048].rearrange("p (a n) -> p a n", a=8)
            T.dma("pool", lambda e: e.dma_start(out=pw, in_=pool_w[j].rearrange("g (k p) n -> p (g k) n", p=128)),
                  sname, writes=[rs])
            Y = av(32 * KB, 8 * 1024, F32).rearrange("p (c n) -> p c n", c=8)
            SQR = av(80 * KB, 4 * 512, BF16).rearrange("p (i n) -> p i n", i=4)
            RSTD2 = av(100 * KB, 1024, F32)
            r_sqr = [Res() for _ in range(4)]
            r_rstd2 = Res()
            r_y = [Res() for _ in range(NCT)]
            psc = [colS("pool_scale", j * 8 + m) for m in range(NCT)]
            T.dma("sp", lambda e: e.dma_start(out=INVCs[1], in_=invc_in[:, :, 1024:2048]), "invc1", writes=[r_invcs[1]])
            for tg in range(2):
                INVC, r_invc = INVCs[tg], r_invcs[tg]

                def mm_pool(m, c, bk, tg=tg):
                    g, m2 = m // 2, m % 2
                    t0 = tg * 1024 + c * 512
                    fns = [lambda e, k2=k2, bk=bk, g=g, m2=m2, t0=t0: e.matmul(
                        bk, pw[:, g * 2 + k2, m2 * 128:(m2 + 1) * 128], DB[:, g * 2 + k2, t0:t0 + 512],
                        start=(k2 == 0), stop=(k2 == 1)) for k2 in range(2)]
                    return fns, [rs, r_db[g * 2][tg * 2 + c], r_db[g * 2 + 1][tg * 2 + c]]
                out_proj(tg, mm_pool, Y, r_y, SQR, r_sqr, RSTD2, r_rstd2, 1, pool_scale_cols=psc, invc=INVC, r_invc=r_invc)
            T.fence()

        OST = av(0, 4 * 1024, F32).rearrange("p (s n) -> p s n", s=4)
        r_ost = [Res() for _ in range(4)]
        ZDONE = set()

        def out_unit(tt, half):
            s = tt % 4
            bk, rb = next_bank()
            fns = [lambda e, q=q, bk=bk: e.transpose(
                bk[:, q * 128:(q + 1) * 128], X[:, half * 4 + q, tt * 128:(tt + 1) * 128], identf_t[:])
                for q in range(4)]
            T.op("pe", fns, reads=[R_X[half * 4 + q][tt // 4] for q in range(4)] + [R_ident], writes=[rb])
            dst = OST[:, s, half * 512:(half + 1) * 512]
            if half == 0:
                T.op("act", lambda e: e.activation(out=dst, in_=bk, func=AF.Copy), reads=[rb], writes=[r_ost[s]])
            else:
                T.op("dve", lambda e: e.tensor_copy(out=dst, in_=bk), reads=[rb], writes=[r_ost[s]])
                T.dma("sp", lambda e: e.dma_start(out=y_out[tt * 128:(tt + 1) * 128, :], in_=OST[:, s, :]),
                      f"st{s}", reads=[r_ost[s]])
                ZDONE.add(tt)

        if plan and plan[0][0] == "mod":
            mod_state["advance"] = True
            mod_state["extra"] = [(av(16 * KB + i * 8 * KB, 4096, BF16), Res(), f"ms{i}") for i in range(6)]
            mod_state["job"] = mod_job(plan[0][1])
            phase_a(lambda: pump(1))
            pump(1000)
            mod_state["extra"] = None
            T.fence()
            plan = plan[1:]
        else:
            phase_a()
        pre_flag = False
        for pi_, ph in enumerate(plan):
            if ph[0] == "mod":
                phase_mod(ph[1])
            elif ph[0] == "ffn":
                nx = plan[pi_ + 1] if pi_ + 1 < len(plan) else None
                pn = nx[1] if (FFN_CHAIN and nx is not None and nx[0] == "ffn" and nx[2] == 0 and ph[2] == 1) else None
                phase_ffn(ph[1], ph[2], pre_done=pre_flag, pre_next=pn,
                          zdrip=(Z_OVERLAP and pi_ == len(plan) - 1 and ph[2] == 1))
                pre_flag = pn is not None
            elif ph[0] == "rg":
                phase_rg(ph[1], ph[2] if len(ph) > 2 else None)
            elif ph[0] == "pool":
                phase_pool(ph[1], ph[2] if len(ph) > 2 else None)

        T.fence()
        for tt in range(16):
            if tt in ZDONE:
                continue
            for half in range(2):
                out_unit(tt, half)
        T.dma("sp", lambda e: e.dma_start(out=st_out[:, :], in_=states), "stst", reads=[R_states])
        T.final_wait("sp")
        if os.environ.get("KDEBUG"):
            print("TRACKER counts", T.count, "semvals", T.semval, "nops", {k: len(v) for k, v in T.ops.items()})

        handles = {"pe": "tensor", "act": "scalar", "dve": "vector", "pool": "gpsimd", "sp": "sync"}

        def replay(name):
            def run(e):
                for o in T.ops[name]:
                    if o[0] == "wait":
                        e.wait_ge(sems[o[1]], o[2])
                    elif o[0] == "op":
                        ins = o[1](e)
                        if o[2]:
                            ins.then_inc(sems[name], 1)
                    else:
                        o[1](e).then_inc(sems[o[2]], 16)
            return run

        block.tensor(replay("pe"))
        block.scalar(replay("act"))
        block.vector(replay("dve"))
        block.gpsimd(replay("pool"))
        block.sync(replay("sp"))
    return nc


def make_in_maps(inp):
    g = lambda k: np.asarray(inp[k], np.float32)
    pall_p, invc_p = _pool_mats(False)
    pall_s, invc_s = _pool_mats(True)
    identf = np.eye(128, dtype=np.float32)
    identb = identf.astype(ml_dtypes.bfloat16)
    shared = {k: np.ascontiguousarray(g(k)) for k in ("mod_w", "ffn_w1", "ffn_w3", "ffn_w2", "rg_w_in", "rg_w_a",
                                                        "rg_w_x", "rg_w_out", "pool_w")}
    xp = g("x_prompt").reshape(4, NT, D)
    xs = g("x_sample")
    st = g("state_rglru")
    in_maps = []
    for core in range(N_CORES):
        if core < 4:
            x = xp[core]
            cvec = g("c_ctx")
            h0 = np.zeros((2, 2, D), np.float32)
            cross = 0.0
            pall, invc = pall_p, invc_p
        else:
            b = (core - 4) % 2
            x = xs[b]
            cvec = g("c")[b]
            h0 = st[b]
            cross = 1.0
            pall, invc = pall_s, invc_s
        small = np.zeros((128, NSMALL), np.float32)

        def put(name, arr):
            a = _fm(arr)
            small[:, SC[name]:SC[name] + a.shape[1]] = a
        put("norm_g", g("norm_g"))
        put("mod_b", g("mod_b").reshape(4, 9, D))
        put("conv_w", g("rg_conv_w"))
        put("conv_b", g("rg_conv_b"))
        put("b_a", g("rg_b_a"))
        put("b_x", g("rg_b_x"))
        put("lam", g("rg_lam"))
        put("pool_scale", g("pool_scale"))
        put("h0", h0)
        put("cvec", cvec)
        small[:, SC["cross"]] = cross
        m = {"x_in": np.ascontiguousarray(x), "small": small, "pall": pall, "invc": invc,
             "identf": identf, "identb": identb}
        m.update(shared)
        in_maps.append(m)
    return in_maps


def assemble(results):
    y_prompt = np.stack([results[c]["y_out"] for c in range(4)]).reshape(32, 256, D).astype(np.float32)
    y_sample = np.stack([results[4 + b]["y_out"] for b in range(2)]).astype(np.float32)
    ns = np.zeros((32, 2, 2, D), np.float32)
    for c in range(4):
        stc = np.asarray(results[c]["st_out"]).reshape(128, 2, 2, 8, 8)
        ns[c * 8:(c + 1) * 8] = stc.transpose(4, 1, 2, 3, 0).reshape(8, 2, 2, D)
    return y_prompt, y_sample, ns


_NC_CACHE = {}


def kernel(**inputs):
    if "nc" not in _NC_CACHE:
        _NC_CACHE["nc"] = build()
    nc = _NC_CACHE["nc"]
    in_maps = make_in_maps(inputs)
    res = run_bass_kernel_spmd(nc, in_maps, core_ids=list(range(N_CORES)))
    return assemble(res.results)
```

```python
import numpy as np
import ml_dtypes
import concourse.bass as bass
import concourse.mybir as mybir
from concourse.bass_utils import run_bass_kernel_spmd
from concourse.ap import AP

F32 = mybir.dt.float32
BF16 = mybir.dt.bfloat16
AF = mybir.ActivationFunctionType
ALU = mybir.AluOpType

D = 1024
DFF = 2816
NT = 2048
NCT = 8
NJ = 22
DEPTH = 4
EPS = 1e-6
POOL_WINDOWS = (2, 4, 8, 16)
GRID_W = 64
N_CORES = 8
import os
DEBUG_STAGE = int(os.environ.get("DEBUG_STAGE", "0"))
RG_STAGE = int(os.environ.get("RG_STAGE", "0"))
FFN_CHAIN = int(os.environ.get("FFN_CHAIN", "1"))
Z_OVERLAP = int(os.environ.get("Z_OVERLAP", "1"))


def _small_cols():
    cols = {}
    off = 0
    for n, k in (("norm_g", 4 * 6 * 8), ("mod_b", 4 * 9 * 8), ("conv_w", 2 * 4 * 8), ("conv_b", 2 * 8),
                 ("b_a", 2 * 2 * 8), ("b_x", 2 * 2 * 8), ("lam", 2 * 2 * 8), ("pool_scale", 2 * 8),
                 ("h0", 2 * 2 * 8), ("cvec", 8), ("cross", 1)):
        cols[n] = off
        off += k
    return cols, off


SC, NSMALL = _small_cols()


def _fm(v):
    v = np.asarray(v, np.float32)
    return np.ascontiguousarray(v.reshape(-1, 8, 128).transpose(2, 0, 1)).reshape(128, -1)


def _ktiles(w, c8):
    r0 = 4 * c8
    lo = max(r0 - w // 2, 0)
    hi = min(r0 + 3 + (w - w // 2) - 1, 31)
    return list(range(lo // 2, hi // 2 + 1))


def _pool_index():
    idx = {}
    off = 0
    for g, w in enumerate(POOL_WINDOWS):
        for c8 in range(8):
            kts = _ktiles(w, c8)
            idx[(g, c8)] = (off, kts)
            off += len(kts)
    return idx, off


PIDX, PTOT = _pool_index()


def _win(n, w):
    t = np.arange(n)
    lo = np.maximum(t - w // 2, 0)
    hi = np.minimum(t + (w - w // 2), n)
    return lo, hi


def _pool_mats(on_grid):
    pall = np.zeros((128, PTOT, 256), np.float32)
    invc = np.zeros((4, NT), np.float32)
    for g, w in enumerate(POOL_WINDOWS):
        P = np.zeros((NT, NT), np.float32)
        if on_grid:
            rows = NT // GRID_W
            rlo, rhi = _win(rows, w)
            clo, chi = _win(GRID_W, w)
            R = np.zeros((rows, rows), np.float32)
            C = np.zeros((GRID_W, GRID_W), np.float32)
            for r in range(rows):
                R[r, rlo[r]:rhi[r]] = 1
            for c in range(GRID_W):
                C[c, clo[c]:chi[c]] = 1
            P = np.kron(R, C)
        else:
            lo, hi = _win(256, w)
            B = np.zeros((256, 256), np.float32)
            for t in range(256):
                B[t, lo[t]:hi[t]] = 1
            for s in range(8):
                P[s * 256:(s + 1) * 256, s * 256:(s + 1) * 256] = B
        cnt = P.sum(axis=1)
        invc[g] = 1.0 / cnt
        PT = P.T.copy()
        PT[np.arange(NT), np.arange(NT)] -= cnt
        for c8 in range(8):
            off, kts = PIDX[(g, c8)]
            cols = slice(c8 * 256, (c8 + 1) * 256)
            nz = np.nonzero(np.abs(PT[:, cols]).sum(axis=1))[0]
            assert all((t // 128) in kts for t in nz), (g, c8)
            for i, kt in enumerate(kts):
                pall[:, off + i, :] = PT[kt * 128:(kt + 1) * 128, cols]
    return pall.astype(ml_dtypes.bfloat16), np.ascontiguousarray(np.broadcast_to(invc[None], (128, 4, NT)))


import types


def _freeze(fn):
    if fn.__closure__ is None:
        return fn
    cells = []
    for c in fn.__closure__:
        try:
            cells.append(types.CellType(c.cell_contents))
        except ValueError:
            cells.append(c)
    return types.FunctionType(fn.__code__, fn.__globals__, fn.__name__, fn.__defaults__, tuple(cells))


class Res:
    __slots__ = ("w", "r")

    def __init__(self):
        self.w = None
        self.r = {}


class Tracker:
    ENGS = ("pe", "act", "dve", "pool", "sp")

    def __init__(self):
        self.ops = {e: [] for e in self.ENGS}
        self.count = {e: 0 for e in self.ENGS}
        self.semval = {}
        self.waited = {e: {} for e in self.ENGS}

    def _wait(self, eng, tok):
        key, val = tok
        if eng == "pe" and key == "pe":
            return
        if self.waited[eng].get(key, 0) >= val:
            return
        self.waited[eng][key] = val
        self.ops[eng].append(("wait", key, val))

    def _deps(self, eng, reads, writes):
        for r in reads:
            if r.w is not None:
                self._wait(eng, r.w)
        for w in writes:
            if w.w is not None:
                self._wait(eng, w.w)
            for k, v in w.r.items():
                self._wait(eng, (k, v))

    def _commit(self, tok, reads, writes):
        k, v = tok
        for r in reads:
            if r.r.get(k, 0) < v:
                r.r[k] = v
        for w in writes:
            w.w = tok
            w.r = {}

    def op(self, eng, fns, reads=(), writes=()):
        if callable(fns):
            fns = [fns]
        fns = [_freeze(f) for f in fns]
        self._deps(eng, reads, writes)
        self.count[eng] += 1
        tok = (eng, self.count[eng])
        for f in fns[:-1]:
            self.ops[eng].append(("op", f, False))
        self.ops[eng].append(("op", fns[-1], True))
        self._commit(tok, reads, writes)
        return tok

    def dma(self, queue, fns, sem, reads=(), writes=()):
        if callable(fns):
            fns = [fns]
        fns = [_freeze(f) for f in fns]
        self._deps(queue, reads, writes)
        self.semval[sem] = self.semval.get(sem, 0) + 16 * len(fns)
        tok = (sem, self.semval[sem])
        for fn in fns:
            self.ops[queue].append(("dma", fn, sem))
        self._commit(tok, reads, writes)
        return tok

    def fence(self, engs=("pe", "act", "dve", "sp")):
        for e in engs:
            for k in ("pe", "act", "dve"):
                if k != e and self.count[k] > 0:
                    self._wait(e, (k, self.count[k]))

    def final_wait(self, eng):
        for k, v in self.semval.items():
            self._wait(eng, (k, v))
        for k in ("pe", "act", "dve"):
            if self.count[k] > 0 and k != eng:
                self._wait(eng, (k, self.count[k]))


def build(plan=None, n_layers=DEPTH):
    if plan is None:
        plan = []
        for l in range(n_layers):
            nxt = l + 1 if l + 1 < n_layers else None
            if l == 0:
                plan.append(("mod", 0))
            plan += [("ffn", l, 0), ("rg", l, nxt) if l % 2 == 0 else ("pool", l, nxt), ("ffn", l, 1)]
    nc = bass.Bass("TRN2", target_bir_lowering=False)
    T = Tracker()

    def din(name, shape, dt=F32):
        return nc.dram_tensor(name, list(shape), dt, kind="ExternalInput").ap()

    x_in = din("x_in", [NT, D])
    small_in = din("small", [128, NSMALL])
    pall_in = din("pall", [128, PTOT, 256], BF16)
    invc_in = din("invc", [128, 4, NT])
    identf_in = din("identf", [128, 128])
    identb_in = din("identb", [128, 128], BF16)
    mod_w = din("mod_w", [4, D, 9 * D])
    ffn_w1 = din("ffn_w1", [4, 2, D, DFF])
    ffn_w3 = din("ffn_w3", [4, 2, D, DFF])
    ffn_w2 = din("ffn_w2", [4, 2, DFF, D])
    rg_w_in = din("rg_w_in", [2, D, 2 * D])
    rg_w_a = din("rg_w_a", [2, 2, 8, 128, 128])
    rg_w_x = din("rg_w_x", [2, 2, 8, 128, 128])
    rg_w_out = din("rg_w_out", [2, D, D])
    pool_w = din("pool_w", [2, 4, 256, 256])
    y_out = nc.dram_tensor("y_out", [NT, D], F32, kind="ExternalOutput").ap()
    st_out = nc.dram_tensor("st_out", [128, 256], F32, kind="ExternalOutput").ap()

    ARENA_E = 53312
    NSLOT = 2
    ctx = [
        nc.sbuf_tensor("sb_X", [128, NCT, NT], F32),
        nc.sbuf_tensor("sb_arena", [128, ARENA_E], BF16),
        nc.sbuf_tensor("sb_ring", [128, NSLOT, 4096], BF16),
        nc.sbuf_tensor("sb_small", [128, NSMALL], F32),
        nc.sbuf_tensor("sb_modt", [128, 2, 72], F32),
        nc.sbuf_tensor("sb_der", [128, 2, 72], F32),
        nc.sbuf_tensor("sb_cl", [128, 32], F32),
        nc.sbuf_tensor("sb_scb", [128, 8], BF16),
        nc.sbuf_tensor("sb_sctmp", [128, 8], F32),
        nc.sbuf_tensor("sb_ones", [128, 128], BF16),
        nc.sbuf_tensor("sb_identf", [128, 128], F32),
        nc.sbuf_tensor("sb_identb", [128, 128], BF16),
        nc.sbuf_tensor("sb_rstd", [128, NT], F32),
        nc.sbuf_tensor("sb_tmps", [128, 3, 512], F32),
        nc.sbuf_tensor("sb_states", [128, 256], F32),
    ]
    psum_ctx = [nc.psum_tensor(f"ps{i}", [128, 512], F32) for i in range(8)]
    sem_names = ["pe", "act", "dve", "pool_c", "ld_c", "ld_x0", "ld_x1", "ld_x2", "ld_x3", "slot0", "slot1",
                 "pl0", "pl1", "pl2", "pl3", "invc", "invc1", "st0", "st1", "st2", "st3", "stst", "modld", "ms0", "ms1", "ms2", "ms3", "ms4", "ms5"]
    sem_ctx = [nc.semaphore(n) for n in sem_names]

    import contextlib
    with contextlib.ExitStack() as es:
        (Xt, arena_t, ring_t, small_t, modt_t, der_t, cl_t, scb_t, sctmp_t, ones_t, identf_t, identb_t,
         rstd_t, tmps_t, states_t) = [es.enter_context(c) for c in ctx]
        banks = [es.enter_context(c) for c in psum_ctx]
        sems = {n: es.enter_context(c) for n, c in zip(sem_names, sem_ctx)}
        block = es.enter_context(nc.Block())

        X = Xt[:]
        arena = arena_t[:]
        small = small_t[:]
        M = {}
        R_mod2 = [Res(), Res()]
        R_der2 = [Res(), Res()]
        R_mb = Res()

        def set_layer(l):
            p = l % 2
            M["modt"] = modt_t[:, p, :]
            M["der"] = der_t[:, p, :]
            M["R_mod"] = R_mod2[p]
            M["R_der"] = R_der2[p]
        set_layer(0)
        cl = cl_t[:]
        rstd = rstd_t[:]
        tmps = tmps_t[:]
        states = states_t[:]
        ones = ones_t[:]

        R_X = [[Res() for _ in range(4)] for _ in range(NCT)]
        R_small = Res()
        R_cl = Res()
        R_scb = Res()
        R_ones = Res()
        R_ident = Res()
        R_rstd = Res()
        R_tmps = [Res() for _ in range(3)]
        R_states = Res()
        R_bank = [Res() for _ in range(8)]
        R_slot = [Res() for _ in range(NSLOT)]
        bank_rr = [0]
        slot_rr = [0]

        RR_BANKS = [0, 1, 2, 3, 4]
        R_st = [Res(), Res()]
        ring_rr = [0]

        def next_bank():
            b = RR_BANKS[bank_rr[0] % len(RR_BANKS)]
            bank_rr[0] += 1
            return banks[b][:], R_bank[b]

        def st_bank(c):
            return banks[5 + c][:], R_st[c]

        def next_slot(advance=True):
            s = slot_rr[0] % NSLOT
            if advance:
                slot_rr[0] += 1
            return ring_t[:, s, :], R_slot[s], f"slot{s}"

        def av(off_bytes, nelem, dt):
            assert off_bytes % 4 == 0
            if dt == BF16:
                assert off_bytes // 2 + nelem <= ARENA_E, (off_bytes, nelem)
                return arena[:, off_bytes // 2: off_bytes // 2 + nelem]
            assert off_bytes // 2 + 2 * nelem <= ARENA_E, (off_bytes, nelem)
            return arena[:, off_bytes // 2: off_bytes // 2 + 2 * nelem].bitcast(F32)

        KB = 1024

        def colS(name, i):
            o = SC[name] + i
            return small[:, o:o + 1]

        T.dma("sp", [lambda e: e.dma_start(out=small, in_=small_in[:, :]),
                     lambda e: e.dma_start(out=identf_t[:], in_=identf_in[:, :]),
                     lambda e: e.dma_start(out=identb_t[:], in_=identb_in[:, :])], "ld_c", writes=[R_small, R_ident])
        T.op("dve", lambda e: e.memset(ones, 1.0), writes=[R_ones])
        T.op("dve", lambda e: e.memset(states, 0.0), writes=[R_states])
        cv = small[:, SC["cvec"]:SC["cvec"] + 8]
        T.op("act", lambda e: e.activation(out=sctmp_t[:], in_=cv, func=AF.Silu), reads=[R_small], writes=[R_scb])
        T.op("dve", lambda e: e.tensor_copy(out=scb_t[:], in_=sctmp_t[:]), reads=[R_scb], writes=[R_scb])
        lamv = small[:, SC["lam"]:SC["lam"] + 32]
        c0, c1, c2, c3 = (tmps_t[:, 0, i * 32:(i + 1) * 32] for i in range(4))
        T.op("act", lambda e: e.activation(out=c0, in_=lamv, func=AF.Exp, scale=-1.0), reads=[R_small], writes=[R_cl])
        T.op("dve", lambda e: e.tensor_scalar_add(out=c1, in0=c0, scalar1=2.0), reads=[R_cl], writes=[R_cl])
        T.op("dve", lambda e: e.reciprocal(out=c1, in_=c1), reads=[R_cl], writes=[R_cl])
        T.op("dve", lambda e: e.tensor_mul(out=c1, in0=c0, in1=c1), reads=[R_cl], writes=[R_cl])
        T.op("dve", lambda e: e.tensor_mul(out=c2, in0=c1, in1=c1), reads=[R_cl], writes=[R_cl])
        T.op("dve", lambda e: e.tensor_scalar(out=c3, in0=c2, scalar1=1.0 / 11.0, scalar2=1.0 / 9.0,
                                              op0=ALU.mult, op1=ALU.add), reads=[R_cl], writes=[R_cl])
        for coef in (1.0 / 7.0, 1.0 / 5.0, 1.0 / 3.0, 1.0):
            T.op("dve", lambda e: e.tensor_mul(out=c3, in0=c3, in1=c2), reads=[R_cl], writes=[R_cl])
            T.op("dve", lambda e, coef=coef: e.tensor_scalar_add(out=c3, in0=c3, scalar1=coef),
                 reads=[R_cl], writes=[R_cl])
        T.op("dve", lambda e: e.tensor_mul(out=c3, in0=c3, in1=c1), reads=[R_cl], writes=[R_cl])
        T.op("dve", lambda e: e.tensor_scalar_mul(out=cl, in0=c3, scalar1=-16.0), reads=[R_cl], writes=[R_cl])

        def phase_a(pump_fn=None):
            STG = av(0, 4 * 1024, F32).rearrange("p (i n) -> p i n", i=4)
            R_stg = [Res() for _ in range(4)]
            for tq in range(4):
                for i in range(4):
                    tt = tq * 4 + i
                    T.dma("sp", lambda e, i=i, tt=tt: e.dma_start(out=STG[:, i, :], in_=x_in[tt * 128:(tt + 1) * 128, :]),
                          f"ld_x{i}", writes=[R_stg[i]])
                for ct in range(NCT):
                    bk, rb = next_bank()
                    fns = [lambda e, i=i, ct=ct, bk=bk: e.transpose(bk[:, i * 128:(i + 1) * 128],
                                                                     STG[:, i, ct * 128:(ct + 1) * 128], identf_t[:])
                           for i in range(4)]
                    T.op("pe", fns, reads=R_stg + [R_ident], writes=[rb])
                    eng = "act" if ct % 2 == 0 else "dve"
                    dst = X[:, ct, tq * 512:(tq + 1) * 512]
                    if eng == "act":
                        T.op("act", lambda e, dst=dst, bk=bk: e.activation(out=dst, in_=bk, func=AF.Copy),
                             reads=[rb], writes=[R_X[ct][tq]])
                    else:
                        T.op("dve", lambda e, dst=dst, bk=bk: e.tensor_copy(out=dst, in_=bk),
                             reads=[rb], writes=[R_X[ct][tq]])
                    if pump_fn is not None:
                        pump_fn()
            T.fence()

        def rstd_from_bank(stb, rst, dst_ap, r_dst, stage):
            if stage == 0:
                T.op("act", lambda e: e.activation(out=dst_ap, in_=stb, func=AF.Ln, scale=1.0 / D, bias=EPS),
                     reads=[rst], writes=[r_dst])
            else:
                T.op("act", lambda e: e.activation(out=dst_ap, in_=dst_ap, func=AF.Exp, scale=-0.5),
                     reads=[r_dst], writes=[r_dst])

        def stats_finish(c, dst_ap, r_dst):
            stb, rst = st_bank(c)
            rstd_from_bank(stb, rst, dst_ap, r_dst, 0)
            rstd_from_bank(stb, rst, dst_ap, r_dst, 1)

        def x_stats_ring(chunks, SQR, r_sqr):
            used = []
            for c in chunks:
                stb, rst = next_bank()
                used.append((c, stb, rst))
                for ct in range(NCT):
                    i = ring_rr[0] % 4
                    ring_rr[0] += 1
                    src = X[:, ct, c * 512:(c + 1) * 512]
                    if ct % 2 == 0:
                        T.op("act", lambda e, i=i, src=src: e.activation(out=SQR[:, i, :], in_=src, func=AF.Square),
                             reads=[R_X[ct][c]], writes=[r_sqr[i]])
                    else:
                        T.op("dve", lambda e, i=i, src=src: e.tensor_mul(out=SQR[:, i, :], in0=src, in1=src),
                             reads=[R_X[ct][c]], writes=[r_sqr[i]])
                    T.op("pe", lambda e, i=i, ct=ct, stb=stb: e.matmul(stb, ones, SQR[:, i, :], start=(ct == 0),
                                                                       stop=(ct == NCT - 1)),
                         reads=[r_sqr[i], R_ones], writes=[rst])
            for stage in range(2):
                for c, stb, rst in used:
                    rstd_from_bank(stb, rst, rstd[:, c * 512:(c + 1) * 512], R_rstd, stage)

        def inherit(news, olds):
            for n in news:
                for o in olds:
                    toks = dict(o.r)
                    if o.w is not None:
                        toks[o.w[0]] = max(toks.get(o.w[0], 0), o.w[1])
                    for k, v in toks.items():
                        if n.r.get(k, 0) < v:
                            n.r[k] = v

        def modulate(chunks, Hout, r_h, sub):
            for ci, c in enumerate(chunks):
                for ct in range(NCT):
                    k = (ci * NCT + ct) % 2
                    tmp = tmps[:, k, :]
                    g_ap = M["der"][:, sub * 24 + ct: sub * 24 + ct + 1]
                    sh_ap = M["modt"][:, sub * 24 + ct: sub * 24 + ct + 1]
                    T.op("dve", lambda e, tmp=tmp, ct=ct, c=c, g_ap=g_ap: e.scalar_tensor_tensor(
                        out=tmp, in0=X[:, ct, c * 512:(c + 1) * 512], scalar=g_ap, in1=rstd[:, c * 512:(c + 1) * 512],
                        op0=ALU.mult, op1=ALU.mult), reads=[R_X[ct][c], M["R_der"], R_rstd], writes=[R_tmps[k]])
                    T.op("act", lambda e, tmp=tmp, ct=ct, ci=ci, sh_ap=sh_ap: e.activation(
                        out=Hout[:, ct, ci * 512:(ci + 1) * 512], in_=tmp, func=AF.Identity, bias=sh_ap, scale=1.0),
                        reads=[R_tmps[k], M["R_mod"]], writes=[r_h[ct][c]])

        def out_proj(tg, mm_emit, Y, r_y, SQR, r_sqr, RSTD2, r_rstd2, sub, pool_scale_cols=None, invc=None, r_invc=None,
                     defer=None, drip_in=None):
            pending = []

            def flush(keep):
                while len(pending) > keep:
                    c, m, i = pending.pop(0)
                    stb, rst = st_bank(c)
                    T.op("pe", lambda e, i=i, m=m, stb=stb: e.matmul(stb, ones, SQR[:, i, :], start=(m == 0),
                                                                     stop=(m == NCT - 1)),
                         reads=[r_sqr[i], R_ones], writes=[rst])

            for m in range(NCT):
                for c in range(2):
                    bk, rb = next_bank()
                    fns, rr = mm_emit(m, c, bk)
                    T.op("pe", fns, reads=rr, writes=[rb])
                    ydst = Y[:, m, c * 512:(c + 1) * 512]
                    i = ring_rr[0] % 4
                    ring_rr[0] += 1
                    if pool_scale_cols is None:
                        T.op("dve", lambda e, ydst=ydst, bk=bk: e.tensor_copy(out=ydst, in_=bk),
                             reads=[rb], writes=[r_y[m]])
                    else:
                        g = m // 2
                        psc = pool_scale_cols[m]
                        iv = invc[:, g, c * 512:(c + 1) * 512]
                        T.op("dve", lambda e, ydst=ydst, bk=bk, psc=psc, iv=iv: e.scalar_tensor_tensor(
                            out=ydst, in0=bk, scalar=psc, in1=iv, op0=ALU.mult, op1=ALU.mult),
                            reads=[rb, R_small, r_invc], writes=[r_y[m]])
                    T.op("act", lambda e, i=i, ydst=ydst: e.activation(out=SQR[:, i, :], in_=ydst, func=AF.Square),
                         reads=[r_y[m]], writes=[r_sqr[i]])
                    pending.append((c, m, i))
                    flush(2)
                    if drip_in:
                        drip_in.pop(0)()
            flush(0)
            while drip_in:
                drip_in.pop(0)()
            tail = []
            for stage in range(2):
                for c in range(2):
                    tail.append(lambda c=c, stage=stage: rstd_from_bank(*st_bank(c), RSTD2[:, c * 512:(c + 1) * 512],
                                                                        r_rstd2, stage))
            r_der_now = M["R_der"]
            for m in range(NCT):
                gg = M["der"][:, sub * 24 + 16 + m: sub * 24 + 16 + m + 1]
                xs = X[:, m, tg * 1024:(tg + 1) * 1024]

                def t1(m=m, gg=gg):
                    T.op("dve", lambda e, m=m, gg=gg: e.scalar_tensor_tensor(
                        out=Y[:, m, :], in0=Y[:, m, :], scalar=gg, in1=RSTD2, op0=ALU.mult, op1=ALU.mult),
                        reads=[r_y[m], r_der_now, r_rstd2], writes=[r_y[m]])

                def t2(m=m, xs=xs):
                    T.op("dve", lambda e, m=m, xs=xs: e.tensor_add(out=xs, in0=xs, in1=Y[:, m, :]),
                         reads=[r_y[m], R_X[m][2 * tg], R_X[m][2 * tg + 1]],
                         writes=[R_X[m][2 * tg], R_X[m][2 * tg + 1]])
                tail += [t1, t2]
            if defer is None:
                for t in tail:
                    t()
            else:
                defer.extend(tail)

        mod_state = {"job": None, "advance": True}

        def mod_job(l):
            p = l % 2
            modt = modt_t[:, p, :]
            der = der_t[:, p, :]
            r_mod, r_der = R_mod2[p], R_der2[p]
            bk, rb = banks[7][:], R_bank[7]
            wv = mod_w[l].rearrange("(k p) n -> p k n", p=128)
            for blk in range(18):
                if mod_state.get("extra"):
                    sl, rs, sname = mod_state["extra"][blk % len(mod_state["extra"])]
                else:
                    sl, rs, sname = next_slot(advance=mod_state["advance"])
                sv = sl.rearrange("p (k n) -> p k n", k=8)
                T.dma("pool", lambda e, sv=sv, blk=blk: e.dma_start(out=sv, in_=wv[:, :, blk * 512:(blk + 1) * 512]),
                      sname, writes=[rs])
                fns = []
                for mi in range(4):
                    col = blk * 4 + mi
                    for k in range(8):
                        fns.append(lambda e, sv=sv, mi=mi, k=k, col=col: e.matmul(
                            bk[:, col:col + 1], sv[:, k, mi * 128:(mi + 1) * 128], scb_t[:, k:k + 1],
                            start=(k == 0), stop=(k == 7)))
                T.op("pe", fns, reads=[rs, R_scb], writes=[rb])
                yield
            mb = small[:, SC["mod_b"] + l * 72: SC["mod_b"] + (l + 1) * 72]
            T.op("dve", lambda e: e.tensor_tensor(out=modt, in0=bk[:, 0:72], in1=mb, op=ALU.add),
                 reads=[rb, R_small], writes=[r_mod])
            for s in range(3):
                ng_a = small[:, SC["norm_g"] + (l * 6 + 2 * s) * 8: SC["norm_g"] + (l * 6 + 2 * s) * 8 + 8]
                ng_b = small[:, SC["norm_g"] + (l * 6 + 2 * s + 1) * 8: SC["norm_g"] + (l * 6 + 2 * s + 1) * 8 + 8]
                T.op("dve", lambda e, s=s, ng_a=ng_a: e.scalar_tensor_tensor(
                    out=der[:, s * 24:s * 24 + 8], in0=modt[:, s * 24 + 8:s * 24 + 16], scalar=1.0, in1=ng_a,
                    op0=ALU.add, op1=ALU.mult), reads=[r_mod, R_small], writes=[r_der])
                T.op("dve", lambda e, s=s, ng_b=ng_b: e.scalar_tensor_tensor(
                    out=der[:, s * 24 + 16:s * 24 + 24], in0=modt[:, s * 24 + 16:s * 24 + 24],
                    scalar=(1.0 if s == 1 else 0.5), in1=ng_b, op0=ALU.mult, op1=ALU.mult),
                    reads=[r_mod, R_small], writes=[r_der])

        def pump(n=1):
            j = mod_state["job"]
            for _ in range(n):
                if j is None:
                    return
                try:
                    next(j)
                except StopIteration:
                    mod_state["job"] = j = None

        def phase_mod(l):
            mod_state["advance"] = True
            mod_state["job"] = mod_job(l)
            pump(1000)

        R_HB = [Res(), Res()]
        FR = {"g": [Res() for _ in range(NJ)], "y": [Res() for _ in range(NCT)], "sa": [Res(), Res()],
              "sqr": [Res() for _ in range(4)], "rstd2": Res()}

        def phase_ffn(l, f, pre_done=False, pre_next=None, zdrip=False):
            set_layer(l)
            sub = 0 if f == 0 else 2
            HB = av(0, 8 * 1024, BF16).rearrange("p (c n) -> p c n", c=8)
            G = av(16 * KB, NJ * 1024, BF16).rearrange("p (j n) -> p j n", j=NJ)
            Y = av(60 * KB, 8 * 1024, F32).rearrange("p (c n) -> p c n", c=8)
            SA = av(92 * KB, 2 * 512, F32).rearrange("p (i n) -> p i n", i=2)
            SQR = av(96 * KB, 4 * 512, BF16).rearrange("p (i n) -> p i n", i=4)
            RSTD2 = av(100 * KB, 1024, F32)
            r_hb = R_HB
            r_g, r_y, r_sa, r_sqr, r_rstd2 = FR["g"], FR["y"], FR["sa"], FR["sqr"], FR["rstd2"]
            w1v = ffn_w1[l, f].rearrange("(k p) n -> p k n", p=128)
            w3v = ffn_w3[l, f].rearrange("(k p) n -> p k n", p=128)
            w2v = ffn_w2[l, f].rearrange("(j p) n -> p j n", p=128)

            def emit_h(tg, sub=sub):
                for ci in range(2):
                    c = 2 * tg + ci
                    for ct in range(NCT):
                        k = (ci * NCT + ct) % 2
                        tmp = tmps[:, k, :]
                        g_ap = M["der"][:, sub * 24 + ct: sub * 24 + ct + 1]
                        sh_ap = M["modt"][:, sub * 24 + ct: sub * 24 + ct + 1]
                        T.op("dve", lambda e, tmp=tmp, ct=ct, c=c, g_ap=g_ap: e.scalar_tensor_tensor(
                            out=tmp, in0=X[:, ct, c * 512:(c + 1) * 512], scalar=g_ap,
                            in1=rstd[:, c * 512:(c + 1) * 512], op0=ALU.mult, op1=ALU.mult),
                            reads=[R_X[ct][c], M["R_der"], R_rstd], writes=[R_tmps[k]])
                        T.op("act", lambda e, tmp=tmp, ct=ct, ci=ci, sh_ap=sh_ap: e.activation(
                            out=HB[:, ct, ci * 512:(ci + 1) * 512], in_=tmp, func=AF.Identity, bias=sh_ap, scale=1.0),
                            reads=[R_tmps[k], M["R_mod"]], writes=[r_hb[ci]])

            if pre_done:
                pass
            else:
                x_stats_ring([0, 1, 2, 3], SQR, r_sqr)
                emit_h(0)
            drip = []
            for tg in range(2):
                if DEBUG_STAGE == 1:
                    continue
                for jb in range(NJ // 2):
                    sl, rs, sname = next_slot()
                    s1 = sl[:, 0:2048].rearrange("p (k n) -> p k n", k=8)
                    s3 = sl[:, 2048:4096].rearrange("p (k n) -> p k n", k=8)
                    T.dma("pool", [lambda e, s1=s1, jb=jb: e.dma_start(out=s1, in_=w1v[:, :, jb * 256:(jb + 1) * 256]),
                                   lambda e, s3=s3, jb=jb: e.dma_start(out=s3, in_=w3v[:, :, jb * 256:(jb + 1) * 256])],
                          sname, writes=[rs])
                    for jj in range(2):
                        j = jb * 2 + jj
                        for c in range(2):
                            bka, rba = next_bank()
                            bkb, rbb = next_bank()
                            fa = [lambda e, k=k, jj=jj, c=c, bka=bka, s1=s1: e.matmul(
                                bka, s1[:, k, jj * 128:(jj + 1) * 128], HB[:, k, c * 512:(c + 1) * 512],
                                start=(k == 0), stop=(k == 7)) for k in range(8)]
                            T.op("pe", fa, reads=[rs, r_hb[c]], writes=[rba])
                            fb = [lambda e, k=k, jj=jj, c=c, bkb=bkb, s3=s3: e.matmul(
                                bkb, s3[:, k, jj * 128:(jj + 1) * 128], HB[:, k, c * 512:(c + 1) * 512],
                                start=(k == 0), stop=(k == 7)) for k in range(8)]
                            T.op("pe", fb, reads=[rs, r_hb[c]], writes=[rbb])
                            si = (j * 2 + c) % 2
                            T.op("act", lambda e, si=si, bka=bka: e.activation(out=SA[:, si, :], in_=bka, func=AF.Silu),
                                 reads=[rba], writes=[r_sa[si]])
                            T.op("dve", lambda e, si=si, bkb=bkb, j=j, c=c: e.tensor_tensor(
                                out=G[:, j, c * 512:(c + 1) * 512], in0=SA[:, si, :], in1=bkb, op=ALU.mult),
                                reads=[r_sa[si], rbb], writes=[r_g[j]])
                            if drip:
                                drip.pop(0)()
                            if pre_done and tg == 0 and jb == 2 and jj == 1 and c == 1:
                                x_stats_ring([2, 3], SQR, r_sqr)
                if DEBUG_STAGE == 2:
                    continue
                if tg == 0:
                    emit_h(1)
                slots = {}

                def mm_down(m, c, bk):
                    if c == 0:
                        sl, rs, sname = next_slot()
                        s2 = sl[:, 0:NJ * 128].rearrange("p (j n) -> p j n", j=NJ)
                        T.dma("pool", [lambda e, s2=s2, m=m, q=q: e.dma_start(
                            out=s2[:, q * 11:(q + 1) * 11, :], in_=w2v[:, q * 11:(q + 1) * 11, m * 128:(m + 1) * 128])
                            for q in range(2)], sname, writes=[rs])
                        slots[m] = (s2, rs)
                    s2, rs = slots[m]
                    fns = [lambda e, j=j, c=c, bk=bk, s2=s2: e.matmul(bk, s2[:, j, :], G[:, j, c * 512:(c + 1) * 512],
                                                                      start=(j == 0), stop=(j == NJ - 1))
                           for j in range(NJ)]
                    return fns, [rs] + r_g

                while drip:
                    drip.pop(0)()
                if tg == 1 and pre_next is not None:
                    set_layer(pre_next)
                    x_stats_ring([0, 1], SQR, r_sqr)
                    emit_h(0, sub=0)
                    set_layer(l)
                zl = None
                if tg == 1 and zdrip:
                    inherit(r_ost, R_HB)
                    zl = [(lambda tt=tt, half=half: out_unit(tt, half)) for tt in range(8) for half in range(2)]
                out_proj(tg, mm_down, Y, r_y, SQR, r_sqr, RSTD2, r_rstd2, sub, defer=(drip if tg == 0 else None),
                         drip_in=zl)
            if pre_next is None:
                T.fence()

        def mixer_h(HM, r_hm):
            SQR = av(32 * KB, 4 * 512, BF16).rearrange("p (i n) -> p i n", i=4)
            r_sqr = [Res() for _ in range(4)]
            x_stats_ring([0, 1, 2, 3], SQR, r_sqr)
            modulate([0, 1, 2, 3], HM, r_hm, 1)
            return r_sqr

        def phase_rg(l, nxt=None):
            set_layer(l)
            if nxt is not None:
                mod_state["advance"] = False
                mod_state["job"] = mod_job(nxt)
            j = l // 2
            HM = av(0, 8 * NT, BF16).rearrange("p (c n) -> p c n", c=8)
            r_hm = [[Res() for _ in range(4)] for _ in range(NCT)]
            r_sqr_mh = mixer_h(HM, r_hm)
            YB = av(32 * KB, 8 * NT, BF16).rearrange("p (c n) -> p c n", c=8)
            AAB = av(64 * KB, NT, F32)
            XRP = av(72 * KB, 8 * 260, F32).rearrange("p (s n) -> p s n", s=8)
            UB = av(72 * KB, NT, F32)
            XC = av(72 * KB + 8320, NT, F32)
            AAF = av(72 * KB + 8320 + 8 * KB, NT, F32)
            UF = av(72 * KB + 8320 + 16 * KB, NT, F32)
            XCB = rstd_t[:, 0:1024].bitcast(BF16)
            T2A = tmps_t[:, 0:2, :].rearrange("p a n -> p (a n)")
            T2B = rstd_t[:, 1024:2048]
            r_yb = [Res() for _ in range(NCT)]
            r_xrp, r_xc, r_aaf, r_aab, r_uf = Res(), Res(), Res(), Res(), Res()
            r_ub = r_xrp
            r_xcb = [Res() for _ in range(4)]
            r_t2 = [Res(), Res()]
            TT = [T2A, T2B]
            inherit(r_yb, r_sqr_mh)
            inherit(r_xcb + [r_t2[1]], [R_rstd])
            inherit([r_t2[0]], [R_tmps[0], R_tmps[1]])
            cross = colS("cross", 0)
            wv = rg_w_in[j].rearrange("(k p) n -> p k n", p=128)
            T.op("dve", lambda e: e.memset(XRP, 0.0), writes=[r_xrp])
            XC3 = XC.rearrange("p (s n) -> p s n", s=8)
            pending_yb = []
            for h in range(8):
                sl, rs, sname = next_slot()
                sg = sl[:, 0:1024].rearrange("p (k n) -> p k n", k=8)
                sx = sl[:, 1024:2048].rearrange("p (k n) -> p k n", k=8)
                dfn = [lambda e, sg=sg, h=h: e.dma_start(out=sg, in_=wv[:, :, h * 128:(h + 1) * 128]),
                       lambda e, sx=sx, h=h: e.dma_start(out=sx, in_=wv[:, :, D + h * 128:D + (h + 1) * 128])]
                gw = {}
                for d in range(2):
                    for gi, wsrc in enumerate((rg_w_a, rg_w_x)):
                        o = 2048 + (d * 2 + gi) * 128
                        gw[(d, gi)] = sl[:, o:o + 128]
                        dfn.append(lambda e, o=o, d=d, h=h, wsrc=wsrc, sl=sl: e.dma_start(
                            out=sl[:, o:o + 128], in_=wsrc[j, d, h]))
                T.dma("pool", dfn, sname, writes=[rs])
                for c in range(4):
                    bk2, rb2 = next_bank()
                    fns = [lambda e, k=k, c=c, bk2=bk2, sx=sx: e.matmul(bk2, sx[:, k, :], HM[:, k, c * 512:(c + 1) * 512],
                                                                        start=(k == 0), stop=(k == 7)) for k in range(8)]
                    T.op("pe", fns, reads=[rs] + [r_hm[k][c] for k in range(NCT)], writes=[rb2])
                    T.op("dve", lambda e, c=c, bk2=bk2: e.tensor_copy(
                        out=XRP[:, 2 * c:2 * c + 2, 1:257], in_=bk2.rearrange("p (s n) -> p s n", s=2)),
                        reads=[rb2], writes=[r_xrp])
                    if c in (1, 3):
                        pump(1)
                T.op("dve", lambda e: e.tensor_scalar_mul(out=XRP[:, 1:8, 0:1], in0=XRP[:, 0:7, 256:257], scalar1=cross),
                     reads=[r_xrp, R_small], writes=[r_xrp])
                T.op("dve", lambda e: e.tensor_scalar_mul(out=XRP[:, 0:7, 257:259], in0=XRP[:, 1:8, 1:3], scalar1=cross),
                     reads=[r_xrp, R_small], writes=[r_xrp])
                T.op("dve", lambda e: e.memset(XRP[:, 0:1, 0:1], 0.0), writes=[r_xrp])
                T.op("dve", lambda e: e.memset(XRP[:, 7:8, 257:260], 0.0), writes=[r_xrp])
                cw = [colS("conv_w", (j * 4 + k) * 8 + h) for k in range(4)]
                cb = colS("conv_b", j * 8 + h)
                T.op("dve", lambda e: e.tensor_scalar(out=XC3, in0=XRP[:, :, 0:256], scalar1=cw[0], scalar2=cb,
                                                      op0=ALU.mult, op1=ALU.add), reads=[r_xrp, R_small], writes=[r_xc])
                for k in range(1, 4):
                    T.op("dve", lambda e, k=k: e.scalar_tensor_tensor(out=XC3, in0=XRP[:, :, k:k + 256], scalar=cw[k],
                                                                      in1=XC3, op0=ALU.mult, op1=ALU.add),
                         reads=[r_xrp, r_xc, R_small], writes=[r_xc])
                for c in range(4):
                    cs = slice(c * 512, (c + 1) * 512)
                    T.op("act", lambda e, cs=cs: e.activation(out=XCB[:, cs], in_=XC[:, cs], func=AF.Copy),
                         reads=[r_xc], writes=[r_xcb[c]])
                while pending_yb:
                    pending_yb.pop(0)()

                def gates(d):
                    AA, r_a = (AAF, r_aaf) if d == 0 else (AAB, r_aab)
                    U, r_u = (UF, r_uf) if d == 0 else (UB, r_ub)
                    ba = colS("b_a", (j * 2 + d) * 8 + h)
                    bx = colS("b_x", (j * 2 + d) * 8 + h)
                    clc = cl[:, (j * 2 + d) * 8 + h:(j * 2 + d) * 8 + h + 1]
                    for c in range(4):
                        cs = slice(c * 512, (c + 1) * 512)
                        bkr, rbr = next_bank()
                        T.op("pe", lambda e, bkr=bkr, cs=cs: e.matmul(bkr, gw[(d, 0)], XCB[:, cs], start=True, stop=True),
                             reads=[rs, r_xcb[c]], writes=[rbr])
                        bki, rbi = next_bank()
                        T.op("pe", lambda e, bki=bki, cs=cs: e.matmul(bki, gw[(d, 1)], XCB[:, cs], start=True, stop=True),
                             reads=[rs, r_xcb[c]], writes=[rbi])
                        T.op("act", lambda e, bkr=bkr, cs=cs: e.activation(out=AA[:, cs], in_=bkr, func=AF.Sigmoid, bias=ba, scale=1.0),
                             reads=[rbr, R_small], writes=[r_a])
                        T.op("act", lambda e, bki=bki, cs=cs: e.activation(out=U[:, cs], in_=bki, func=AF.Sigmoid, bias=bx, scale=1.0),
                             reads=[rbi, R_small], writes=[r_u])
                    T.op("act", lambda e: e.activation(out=AA, in_=AA, func=AF.Exp, scale=clc), reads=[r_a, R_cl], writes=[r_a])

                def a2(d, half):
                    AA, r_a = (AAF, r_aaf) if d == 0 else (AAB, r_aab)
                    hs = slice(half * 1024, (half + 1) * 1024)
                    Tb, rt = TT[half], r_t2[half]
                    T.op("dve", lambda e: e.tensor_mul(out=Tb, in0=AA[:, hs], in1=AA[:, hs]), reads=[r_a], writes=[rt])

                def sq(half):
                    Tb, rt = TT[half], r_t2[half]
                    T.op("act", lambda e: e.activation(out=Tb, in_=Tb, func=AF.Sqrt, scale=-1.0, bias=1.0),
                         reads=[rt], writes=[rt])

                def muls(d, half):
                    U, r_u = (UF, r_uf) if d == 0 else (UB, r_ub)
                    hs = slice(half * 1024, (half + 1) * 1024)
                    Tb, rt = TT[half], r_t2[half]
                    T.op("dve", lambda e: e.tensor_mul(out=U[:, hs], in0=U[:, hs], in1=Tb), reads=[r_u, rt], writes=[r_u])
                    T.op("dve", lambda e: e.tensor_mul(out=U[:, hs], in0=U[:, hs], in1=XC[:, hs]), reads=[r_u, r_xc], writes=[r_u])

                def scan(d):
                    AA, r_a = (AAF, r_aaf) if d == 0 else (AAB, r_aab)
                    U, r_u = (UF, r_uf) if d == 0 else (UB, r_ub)
                    h0c = colS("h0", (j * 2 + d) * 8 + h)
                    if d == 0:
                        ucol, acol, amask = U[:, 0:1], AA[:, 0:1], AA[:, 256::256]
                    else:
                        ucol, acol, amask = U[:, NT - 1:NT], AA[:, NT - 1:NT], AA[:, 255:1792:256]
                    T.op("dve", lambda e: e.scalar_tensor_tensor(out=ucol, in0=acol, scalar=h0c, in1=ucol,
                                                                 op0=ALU.mult, op1=ALU.add),
                         reads=[r_a, r_u, R_small], writes=[r_u])
                    T.op("dve", lambda e: e.tensor_scalar_mul(out=amask, in0=amask, scalar1=cross),
                         reads=[r_a, R_small], writes=[r_a])
                    if d == 0:
                        T.op("dve", lambda e: e.tensor_tensor_scan(out=U, data0=AA, data1=U, initial=0.0,
                                                                   op0=ALU.mult, op1=ALU.add), reads=[r_a, r_u], writes=[r_u])
                        fin = U[:, 255::256]
                    else:
                        T.op("dve", lambda e: e.tensor_tensor_scan(out=U[:, ::-1], data0=AA[:, ::-1], data1=U[:, ::-1],
                                                                   initial=0.0, op0=ALU.mult, op1=ALU.add),
                             reads=[r_a, r_u], writes=[r_u])
                        fin = U[:, 0::256]
                    so = ((j * 2 + d) * 8 + h) * 8
                    T.op("act", lambda e: e.activation(out=states[:, so:so + 8], in_=fin, func=AF.Copy),
                         reads=[r_u], writes=[R_states])

                gates(0)
                pump(1)
                a2(0, 0); a2(0, 1); sq(0); sq(1)
                gates(1)
                muls(0, 0); muls(0, 1); scan(0)
                a2(1, 0); a2(1, 1); sq(0); sq(1)
                muls(1, 0); muls(1, 1); scan(1)
                T.op("dve", lambda e: e.tensor_add(out=UF, in0=UF, in1=UB), reads=[r_uf, r_ub], writes=[r_uf])
                for c in range(4):
                    cs = slice(c * 512, (c + 1) * 512)
                    bk, rb = next_bank()
                    fns = [lambda e, k=k, c=c, bk=bk, sg=sg: e.matmul(bk, sg[:, k, :], HM[:, k, c * 512:(c + 1) * 512],
                                                                      start=(k == 0), stop=(k == 7)) for k in range(8)]
                    T.op("pe", fns, reads=[rs] + [r_hm[k][c] for k in range(NCT)], writes=[rb])
                    Gt = TT[c // 2][:, (c % 2) * 512:(c % 2 + 1) * 512]
                    rt = r_t2[c // 2]
                    T.op("act", lambda e, bk=bk, Gt=Gt: e.activation(out=Gt, in_=bk, func=AF.Gelu_apprx_tanh),
                         reads=[rb], writes=[rt])
                    pending_yb.append(lambda cs=cs, Gt=Gt, h=h, rt=rt: T.op(
                        "dve", lambda e: e.tensor_mul(out=YB[:, h, cs], in0=UF[:, cs], in1=Gt),
                        reads=[r_uf, rt], writes=[r_yb[h]]))
            while pending_yb:
                pending_yb.pop(0)()
            pump(1000)
            T.fence()
            if RG_STAGE and RG_STAGE < 7:
                return
            wo = rg_w_out[j].rearrange("(k p) n -> p k n", p=128)
            wsl = []
            for half in range(2):
                sl, rs, sname = next_slot()
                sv = sl.rearrange("p (k n) -> p k n", k=8)
                T.dma("pool", lambda e, sv=sv, half=half: e.dma_start(out=sv, in_=wo[:, :, half * 512:(half + 1) * 512]),
                      sname, writes=[rs])
                wsl.append((sv, rs))
            Ys = [av(0, 8 * 1024, F32).rearrange("p (c n) -> p c n", c=8),
                  av(64 * KB, 8 * 1024, F32).rearrange("p (c n) -> p c n", c=8)]
            SQR = av(96 * KB, 4 * 512, BF16).rearrange("p (i n) -> p i n", i=4)
            RSTD2 = av(100 * KB, 1024, F32)
            r_ys = [[Res() for _ in range(NCT)] for _ in range(2)]
            r_sqr = [Res() for _ in range(4)]
            r_rstd2s = [Res(), Res()]
            RSTD2s = [av(100 * KB, 1024, F32), rstd_t[:, 0:1024]]
            mix_drip = []
            for tg in range(2):
                def mm_out(m, c, bk, tg=tg):
                    sv, rs = wsl[m // 4]
                    mo = (m % 4) * 128
                    t0 = tg * 1024 + c * 512
                    fns = [lambda e, k=k, bk=bk, sv=sv, mo=mo, t0=t0: e.matmul(bk, sv[:, k, mo:mo + 128], YB[:, k, t0:t0 + 512],
                                                                               start=(k == 0), stop=(k == 7))
                           for k in range(8)]
                    return fns, [rs] + r_yb
                if RG_STAGE == 7 and tg == 1:
                    continue
                out_proj(tg, mm_out, Ys[tg], r_ys[tg], SQR, r_sqr, RSTD2s[tg], r_rstd2s[tg], 1,
                         defer=(mix_drip if tg == 0 else None), drip_in=(mix_drip if tg == 1 else None))
            T.fence()

        def phase_pool(l, nxt=None):
            set_layer(l)
            if nxt is not None:
                mod_state["advance"] = True
                mod_state["job"] = mod_job(nxt)
            j = l // 2
            HM = av(0, 8 * NT, BF16).rearrange("p (c n) -> p c n", c=8)
            r_hm = [[Res() for _ in range(4)] for _ in range(NCT)]
            r_sqr_mh = mixer_h(HM, r_hm)
            INVCs = [av(84 * KB, 4 * 1024, F32).rearrange("p (g n) -> p g n", g=4),
                     av(64 * KB, 4 * 1024, F32).rearrange("p (g n) -> p g n", g=4)]
            r_invcs = [Res(), Res()]
            T.dma("sp", lambda e: e.dma_start(out=INVCs[0], in_=invc_in[:, :, 0:1024]), "invc", writes=[r_invcs[0]])
            HT = av(32 * KB, 16 * 1024, BF16).rearrange("p (t n) -> p t n", t=16)
            r_ht = [Res() for _ in range(16)]
            inherit(r_ht, r_sqr_mh)
            for tt in range(16):
                bk, rb = next_bank()
                bkb = bk.bitcast(BF16)
                fns = [lambda e, ct=ct, tt=tt, bkb=bkb: e.transpose(bkb[:, ct * 128:(ct + 1) * 128],
                                                                     HM[:, ct, tt * 128:(tt + 1) * 128], identb_t[:])
                       for ct in range(NCT)]
                T.op("pe", fns, reads=[r_hm[ct][tt // 4] for ct in range(NCT)] + [R_ident], writes=[rb])
                if tt % 2 == 0:
                    T.op("act", lambda e, tt=tt, bkb=bkb: e.activation(out=HT[:, tt, :], in_=bkb, func=AF.Copy),
                         reads=[rb], writes=[r_ht[tt]])
                else:
                    T.op("dve", lambda e, tt=tt, bkb=bkb: e.tensor_copy(out=HT[:, tt, :], in_=bkb),
                         reads=[rb], writes=[r_ht[tt]])
            DB = HM
            r_db = r_hm
            NPS = 4
            PS = av(64 * KB, NPS * 10 * 256, BF16).rearrange("p (s k n) -> p s k n", s=NPS, k=10)
            r_ps = [Res() for _ in range(NPS)]
            pi = 0
            for g in range(4):
                for c8 in range(8):
                    off, kts = PIDX[(g, c8)]
                    nk = len(kts)
                    s = pi % NPS
                    pi += 1
                    T.dma("sp", lambda e, s=s, off=off, nk=nk: e.dma_start(out=PS[:, s, 0:nk, :], in_=pall_in[:, off:off + nk, :]),
                          f"pl{s}", writes=[r_ps[s]])
                    for ct2 in range(2):
                        ct = 2 * g + ct2
                        bk, rb = next_bank()
                        fns = [lambda e, i=i, kt=kt, ct=ct, s=s, bk=bk: e.matmul(
                            bk[:, 0:256], HT[:, kt, ct * 128:(ct + 1) * 128], PS[:, s, i, :],
                            start=(i == 0), stop=(i == nk - 1)) for i, kt in enumerate(kts)]
                        T.op("pe", fns, reads=[r_ps[s]] + [r_ht[kt] for kt in kts], writes=[rb])
                        dst = DB[:, ct, c8 * 256:(c8 + 1) * 256]
                        if ct2 == 0:
                            T.op("act", lambda e, dst=dst, bk=bk: e.activation(out=dst, in_=bk[:, 0:256], func=AF.Copy),
                                 reads=[rb], writes=[r_db[ct][c8 // 2]])
                        else:
                            T.op("dve", lambda e, dst=dst, bk=bk: e.tensor_copy(out=dst, in_=bk[:, 0:256]),
                                 reads=[rb], writes=[r_db[ct][c8 // 2]])
                        if ct2 == 1:
                            pump(1)
            pump(1000)
            T.fence()
            sl, rs, sname = next_slot()
            pw = sl[:, 0:2048].rearrange("p (a n) -> p a n", a=8)
            T.dma("pool", lambda e: e.dma_start(out=pw, in_=pool_w[j].rearrange("g (k p) n -> p (g k) n", p=128)),
                  sname, writes=[rs])
            Y = av(32 * KB, 8 * 1024, F32).rearrange("p (c n) -> p c n", c=8)
            SQR = av(80 * KB, 4 * 512, BF16).rearrange("p (i n) -> p i n", i=4)
            RSTD2 = av(100 * KB, 1024, F32)
            r_sqr = [Res() for _ in range(4)]
            r_rstd2 = Res()
            r_y = [Res() for _ in range(NCT)]
            psc = [colS("pool_scale", j * 8 + m) for m in range(NCT)]
            T.dma("sp", lambda e: e.dma_start(out=INVCs[1], in_=invc_in[:, :, 1024:2048]), "invc1", writes=[r_invcs[1]])
            for tg in range(2):
                INVC, r_invc = INVCs[tg], r_invcs[tg]

                def mm_pool(m, c, bk, tg=tg):
                    g, m2 = m // 2, m % 2
                    t0 = tg * 1024 + c * 512
                    fns = [lambda e, k2=k2, bk=bk, g=g, m2=m2, t0=t0: e.matmul(
                        bk, pw[:, g * 2 + k2, m2 * 128:(m2 + 1) * 128], DB[:, g * 2 + k2, t0:t0 + 512],
                        start=(k2 == 0), stop=(k2 == 1)) for k2 in range(2)]
                    return fns, [rs, r_db[g * 2][tg * 2 + c], r_db[g * 2 + 1][tg * 2 + c]]
                out_proj(tg, mm_pool, Y, r_y, SQR, r_sqr, RSTD2, r_rstd2, 1, pool_scale_cols=psc, invc=INVC, r_invc=r_invc)
            T.fence()

        OST = av(0, 4 * 1024, F32).rearrange("p (s n) -> p s n", s=4)
        r_ost = [Res() for _ in range(4)]
        ZDONE = set()

        def out_unit(tt, half):
            s = tt % 4
            bk, rb = next_bank()
            fns = [lambda e, q=q, bk=bk: e.transpose(
                bk[:, q * 128:(q + 1) * 128], X[:, half * 4 + q, tt * 128:(tt + 1) * 128], identf_t[:])
                for q in range(4)]
            T.op("pe", fns, reads=[R_X[half * 4 + q][tt // 4] for q in range(4)] + [R_ident], writes=[rb])
            dst = OST[:, s, half * 512:(half + 1) * 512]
            if half == 0:
                T.op("act", lambda e: e.activation(out=dst, in_=bk, func=AF.Copy), reads=[rb], writes=[r_ost[s]])
            else:
                T.op("dve", lambda e: e.tensor_copy(out=dst, in_=bk), reads=[rb], writes=[r_ost[s]])
                T.dma("sp", lambda e: e.dma_start(out=y_out[tt * 128:(tt + 1) * 128, :], in_=OST[:, s, :]),
                      f"st{s}", reads=[r_ost[s]])
                ZDONE.add(tt)

        if plan and plan[0][0] == "mod":
            mod_state["advance"] = True
            mod_state["extra"] = [(av(16 * KB + i * 8 * KB, 4096, BF16), Res(), f"ms{i}") for i in range(6)]
            mod_state["job"] = mod_job(plan[0][1])
            phase_a(lambda: pump(1))
            pump(1000)
            mod_state["extra"] = None
            T.fence()
            plan = plan[1:]
        else:
            phase_a()
        pre_flag = False
        for pi_, ph in enumerate(plan):
            if ph[0] == "mod":
                phase_mod(ph[1])
            elif ph[0] == "ffn":
                nx = plan[pi_ + 1] if pi_ + 1 < len(plan) else None
                pn = nx[1] if (FFN_CHAIN and nx is not None and nx[0] == "ffn" and nx[2] == 0 and ph[2] == 1) else None
                phase_ffn(ph[1], ph[2], pre_done=pre_flag, pre_next=pn,
                          zdrip=(Z_OVERLAP and pi_ == len(plan) - 1 and ph[2] == 1))
                pre_flag = pn is not None
            elif ph[0] == "rg":
                phase_rg(ph[1], ph[2] if len(ph) > 2 else None)
            elif ph[0] == "pool":
                phase_pool(ph[1], ph[2] if len(ph) > 2 else None)

        T.fence()
        for tt in range(16):
            if tt in ZDONE:
                continue
            for half in range(2):
                out_unit(tt, half)
        T.dma("sp", lambda e: e.dma_start(out=st_out[:, :], in_=states), "stst", reads=[R_states])
        T.final_wait("sp")
        if os.environ.get("KDEBUG"):
            print("TRACKER counts", T.count, "semvals", T.semval, "nops", {k: len(v) for k, v in T.ops.items()})

        handles = {"pe": "tensor", "act": "scalar", "dve": "vector", "pool": "gpsimd", "sp": "sync"}

        def replay(name):
            def run(e):
                for o in T.ops[name]:
                    if o[0] == "wait":
                        e.wait_ge(sems[o[1]], o[2])
                    elif o[0] == "op":
                        ins = o[1](e)
                        if o[2]:
                            ins.then_inc(sems[name], 1)
                    else:
                        o[1](e).then_inc(sems[o[2]], 16)
            return run

        block.tensor(replay("pe"))
        block.scalar(replay("act"))
        block.vector(replay("dve"))
        block.gpsimd(replay("pool"))
        block.sync(replay("sp"))
    return nc


def make_in_maps(inp):
    g = lambda k: np.asarray(inp[k], np.float32)
    pall_p, invc_p = _pool_mats(False)
    pall_s, invc_s = _pool_mats(True)
    identf = np.eye(128, dtype=np.float32)
    identb = identf.astype(ml_dtypes.bfloat16)
    shared = {k: np.ascontiguousarray(g(k)) for k in ("mod_w", "ffn_w1", "ffn_w3", "ffn_w2", "rg_w_in", "rg_w_a",
                                                        "rg_w_x", "rg_w_out", "pool_w")}
    xp = g("x_prompt").reshape(4, NT, D)
    xs = g("x_sample")
    st = g("state_rglru")
    in_maps = []
    for core in range(N_CORES):
        if core < 4:
            x = xp[core]
            cvec = g("c_ctx")
            h0 = np.zeros((2, 2, D), np.float32)
            cross = 0.0
            pall, invc = pall_p, invc_p
        else:
            b = (core - 4) % 2
            x = xs[b]
            cvec = g("c")[b]
            h0 = st[b]
            cross = 1.0
            pall, invc = pall_s, invc_s
        small = np.zeros((128, NSMALL), np.float32)

        def put(name, arr):
            a = _fm(arr)
            small[:, SC[name]:SC[name] + a.shape[1]] = a
        put("norm_g", g("norm_g"))
        put("mod_b", g("mod_b").reshape(4, 9, D))
        put("conv_w", g("rg_conv_w"))
        put("conv_b", g("rg_conv_b"))
        put("b_a", g("rg_b_a"))
        put("b_x", g("rg_b_x"))
        put("lam", g("rg_lam"))
        put("pool_scale", g("pool_scale"))
        put("h0", h0)
        put("cvec", cvec)
        small[:, SC["cross"]] = cross
        m = {"x_in": np.ascontiguousarray(x), "small": small, "pall": pall, "invc": invc,
             "identf": identf, "identb": identb}
        m.update(shared)
        in_maps.append(m)
    return in_maps


def assemble(results):
    y_prompt = np.stack([results[c]["y_out"] for c in range(4)]).reshape(32, 256, D).astype(np.float32)
    y_sample = np.stack([results[4 + b]["y_out"] for b in range(2)]).astype(np.float32)
    ns = np.zeros((32, 2, 2, D), np.float32)
    for c in range(4):
        stc = np.asarray(results[c]["st_out"]).reshape(128, 2, 2, 8, 8)
        ns[c * 8:(c + 1) * 8] = stc.transpose(4, 1, 2, 3, 0).reshape(8, 2, 2, D)
    return y_prompt, y_sample, ns


_NC_CACHE = {}


def kernel(**inputs):
    if "nc" not in _NC_CACHE:
        _NC_CACHE["nc"] = build()
    nc = _NC_CACHE["nc"]
    in_maps = make_in_maps(inputs)
    res = run_bass_kernel_spmd(nc, in_maps, core_ids=list(range(N_CORES)))
    return assemble(res.results)
```
